# Optimizing a Trainium2 kernel written in Bass

```python
import math
import jax, jax.numpy as jnp
from jax import lax
import numpy as np

D_MODEL = 2048
BATCH = 8
SEQ = 2048
DEPTH = 1

HEAD_DIM = 64
BLOCK = 128
SWA_WINDOW = 128
SWA_Q_HEADS = (D_MODEL // 2) // HEAD_DIM
SWA_KV_HEADS = max(1, SWA_Q_HEADS // 8)
SWA_GROUP = SWA_Q_HEADS // SWA_KV_HEADS
DIFF_V_DIM = 2 * HEAD_DIM
DIFF_HEADS = (D_MODEL // 2) // DIFF_V_DIM
D_FF = -(-(8 * D_MODEL) // (3 * 256)) * 256
PLE_DIM = 256
EPS = 1e-6

W_SWA_Q = SWA_Q_HEADS * HEAD_DIM
W_SWA_KV = SWA_KV_HEADS * HEAD_DIM
W_DIFF_QK = DIFF_HEADS * 2 * HEAD_DIM
W_DIFF_V = DIFF_HEADS * DIFF_V_DIM
IN_WIDTH = W_SWA_Q + 2 * W_SWA_KV + 2 * W_DIFF_QK + W_DIFF_V
SPLITS = tuple(int(c) for c in np.cumsum([W_SWA_Q, W_SWA_KV, W_SWA_KV, W_DIFF_QK, W_DIFF_QK]))
MIX_WIDTH = W_SWA_Q + W_DIFF_V

kernel_name = "hybrid_swa_sink_diff_attn_block"


def rms_norm(x, g):
    x32 = x.astype(jnp.float32)
    y = x32 * lax.rsqrt(jnp.mean(x32 * x32, axis=-1, keepdims=True) + EPS)
    return (y * g.astype(jnp.float32)).astype(x.dtype)


def alibi_slopes(n):
    return jnp.asarray([2.0 ** (-8.0 * (h + 1) / n) for h in range(n)], dtype=jnp.float32)


def swa_sink_attention(q, k, v, sinks, slopes):
    b, s, _, d = q.shape
    nb = s // BLOCK
    qb = q.reshape(b, nb, BLOCK, SWA_KV_HEADS, SWA_GROUP, d)

    def window(t):
        tb = t.reshape(b, nb, BLOCK, SWA_KV_HEADS, d)
        prev = jnp.pad(tb, ((0, 0), (1, 0), (0, 0), (0, 0), (0, 0)))[:, :-1]
        return jnp.concatenate([prev, tb], axis=2)

    kw, vw = window(k), window(v)
    scores = jnp.einsum('bnqhgd,bnkhd->bnhgqk', qb, kw,
                        preferred_element_type=jnp.float32) * (1.0 / math.sqrt(d))
    qi = jnp.arange(BLOCK)[:, None]
    kj = jnp.arange(2 * BLOCK)[None, :]
    dist = qi - kj + BLOCK
    key_pos = jnp.arange(nb)[:, None] * BLOCK - BLOCK + jnp.arange(2 * BLOCK)[None, :]
    valid = ((dist >= 0) & (dist < SWA_WINDOW))[None] & (key_pos >= 0)[:, None, :]
    bias = -slopes.reshape(SWA_KV_HEADS, SWA_GROUP)[:, :, None, None] * dist.astype(jnp.float32)
    logits = jnp.where(valid[None, :, None, None], scores + bias[None, None], -jnp.inf)
    sink = sinks.astype(jnp.float32).reshape(SWA_KV_HEADS, SWA_GROUP)[None, None, :, :, None, None]
    m = jnp.maximum(jnp.max(logits, axis=-1, keepdims=True), sink)
    e = jnp.exp(logits - m)
    probs = e / (jnp.sum(e, axis=-1, keepdims=True) + jnp.exp(sink - m))
    out = jnp.einsum('bnhgqk,bnkhd->bnqhgd', probs, vw.astype(jnp.float32))
    return out.reshape(b, s, SWA_Q_HEADS * d).astype(q.dtype)


def diff_attention(q, k, v, lam, slopes):
    b, s = q.shape[:2]
    nb = s // BLOCK
    qb = jnp.moveaxis(q.reshape(b, nb, BLOCK, DIFF_HEADS, 2, HEAD_DIM), 1, 0)
    key_pos = jnp.arange(s)
    scale = 1.0 / math.sqrt(HEAD_DIM)
    v32 = v.astype(jnp.float32)

    def one_block(args):
        qblk, n = args
        q_pos = n * BLOCK + jnp.arange(BLOCK)
        dist = (q_pos[:, None] - key_pos[None, :]).astype(jnp.float32)
        sc = jnp.einsum('bqhcd,bkhcd->bhcqk', qblk, k,
                        preferred_element_type=jnp.float32) * scale
        sc = jnp.where(dist >= 0, sc - slopes[:, None, None, None] * dist, -jnp.inf)
        pr = jax.nn.softmax(sc, axis=-1)
        attn = pr[:, :, 0] - lam * pr[:, :, 1]
        return jnp.einsum('bhqk,bkhe->bqhe', attn, v32)

    out = lax.map(one_block, (qb, jnp.arange(nb)))
    return jnp.moveaxis(out, 0, 1).reshape(b, s, DIFF_HEADS, DIFF_V_DIM)


def setup_inputs(seed: int = 0) -> dict:
    key = jax.random.key(seed)
    ks = jax.random.split(key, 24)
    f32 = jnp.float32

    def w(k, shape, fan_in):
        return jax.random.normal(k, shape, f32) * (fan_in ** -0.5)

    def gain(k, shape):
        return 1.0 + 0.01 * jax.random.normal(k, shape, f32)

    return {
        "x": jax.random.normal(ks[0], (BATCH, SEQ, D_MODEL), f32),
        "p": jax.random.normal(ks[1], (DEPTH, BATCH, SEQ, PLE_DIM), f32),
        "g_attn": gain(ks[2], (DEPTH, D_MODEL)),
        "w_in": w(ks[3], (DEPTH, D_MODEL, IN_WIDTH), D_MODEL),
        "qn_swa": gain(ks[4], (DEPTH, HEAD_DIM)),
        "kn_swa": gain(ks[5], (DEPTH, HEAD_DIM)),
        "sinks": jax.random.normal(ks[6], (DEPTH, SWA_Q_HEADS), f32),
        "qn_diff": gain(ks[7], (DEPTH, HEAD_DIM)),
        "kn_diff": gain(ks[8], (DEPTH, HEAD_DIM)),
        "lambda_q1": 0.1 * jax.random.normal(ks[9], (DEPTH, HEAD_DIM), f32),
        "lambda_k1": 0.1 * jax.random.normal(ks[10], (DEPTH, HEAD_DIM), f32),
        "lambda_q2": 0.1 * jax.random.normal(ks[11], (DEPTH, HEAD_DIM), f32),
        "lambda_k2": 0.1 * jax.random.normal(ks[12], (DEPTH, HEAD_DIM), f32),
        "g_sub": gain(ks[13], (DEPTH, DIFF_V_DIM)),
        "w_out": w(ks[14], (DEPTH, MIX_WIDTH, D_MODEL), MIX_WIDTH),
        "g_ffn": gain(ks[15], (DEPTH, D_MODEL)),
        "w_gate": w(ks[16], (DEPTH, D_MODEL, D_FF), D_MODEL),
        "w_up": w(ks[17], (DEPTH, D_MODEL, D_FF), D_MODEL),
        "w_down": w(ks[18], (DEPTH, D_FF, D_MODEL), D_FF),
        "g_ple": gain(ks[19], (DEPTH, D_MODEL)),
        "w_ple_gate": w(ks[20], (DEPTH, D_MODEL, D_MODEL), D_MODEL),
        "w_ple_proj": w(ks[21], (DEPTH, PLE_DIM, D_MODEL), PLE_DIM),
        "g_ple_out": gain(ks[22], (DEPTH, D_MODEL)),
    }


def reference(x, p, g_attn, w_in, qn_swa, kn_swa, sinks, qn_diff, kn_diff,
              lambda_q1, lambda_k1, lambda_q2, lambda_k2, g_sub, w_out,
              g_ffn, w_gate, w_up, w_down, g_ple, w_ple_gate, w_ple_proj, g_ple_out):
    b, s, _ = x.shape
    slopes_swa = alibi_slopes(SWA_Q_HEADS)
    slopes_diff = alibi_slopes(DIFF_HEADS)
    h = x
    for i in range(DEPTH):
        lam_init = 0.8 - 0.6 * math.exp(-0.3 * i)
        u = rms_norm(h, g_attn[i])
        z = u @ w_in[i]
        qa, ka, va, qd, kd, vd = jnp.split(z, SPLITS, axis=-1)
        qa = rms_norm(qa.reshape(b, s, SWA_Q_HEADS, HEAD_DIM), qn_swa[i])
        ka = rms_norm(ka.reshape(b, s, SWA_KV_HEADS, HEAD_DIM), kn_swa[i])
        va = va.reshape(b, s, SWA_KV_HEADS, HEAD_DIM)
        ya = swa_sink_attention(qa, ka, va, sinks[i], slopes_swa)

        qd = rms_norm(qd.reshape(b, s, DIFF_HEADS, 2, HEAD_DIM), qn_diff[i])
        kd = rms_norm(kd.reshape(b, s, DIFF_HEADS, 2, HEAD_DIM), kn_diff[i])
        vd = vd.reshape(b, s, DIFF_HEADS, DIFF_V_DIM)
        lam = (jnp.exp(jnp.sum(lambda_q1[i].astype(jnp.float32) * lambda_k1[i].astype(jnp.float32)))
               - jnp.exp(jnp.sum(lambda_q2[i].astype(jnp.float32) * lambda_k2[i].astype(jnp.float32)))
               + lam_init)
        od = diff_attention(qd, kd, vd, lam, slopes_diff)
        yd = (rms_norm(od, g_sub[i]) * (1.0 - lam_init)).reshape(b, s, W_DIFF_V).astype(x.dtype)

        h = h + jnp.concatenate([ya, yd], axis=-1) @ w_out[i]
        u2 = rms_norm(h, g_ffn[i])
        h = h + (jax.nn.silu(u2 @ w_gate[i]) * (u2 @ w_up[i])) @ w_down[i]
        gate = jax.nn.sigmoid(rms_norm(h, g_ple[i]) @ w_ple_gate[i])
        h = h + gate * rms_norm(p[i] @ w_ple_proj[i], g_ple_out[i])
    return h
```

```python
import numpy as np
from contextlib import ExitStack
import concourse.bass as bass
import concourse.mybir as mybir
from concourse.bass_utils import run_bass_kernel_spmd

F32 = mybir.dt.float32
BF16 = mybir.dt.bfloat16
ALU = mybir.AluOpType
AF = mybir.ActivationFunctionType

S = 2048
D = 2048
DFF = 5632
NF = DFF // 128
EPS = 1e-6
NEG = -30000.0
COMPUTE = ('pe', 'act', 'dve', 'pool', 'sp')


class Prog:
    def __init__(self, nc):
        self.nc = nc
        self.engs = {e: [] for e in COMPUTE}
        self.lanes = {e: [] for e in COMPUTE}
        self.res_w = {}
        self.res_r = {}
        self.bar = set()

    def barrier(self):
        self.bar = {(ln, len(lst) - 1) for ln, lst in self.lanes.items() if lst}

    def op(self, eng, fn, reads=(), writes=(), lane=None):
        deps = set()
        for r in reads:
            w = self.res_w.get(r)
            if w is not None:
                deps.add(w)
        for w in writes:
            lw = self.res_w.get(w)
            if lw is not None:
                deps.add(lw)
            for rd in self.res_r.get(w, ()):
                deps.add(rd)
        ln = lane or eng
        if ln not in self.lanes:
            self.lanes[ln] = []
        nd = set()
        for (dl, di) in deps:
            if dl not in COMPUTE:
                if dl == ln:
                    continue
                di = len(self.lanes[dl]) - 1
            nd.add((dl, di))
        deps = nd | self.bar
        idx = len(self.lanes[ln])
        ent = dict(eng=eng, fn=fn, deps=deps, lane=ln, idx=idx, flag=(lane is not None),
                   dma=(lane is not None))
        self.lanes[ln].append(ent)
        self.engs[eng].append(ent)
        me = (ln, idx)
        for r in reads:
            self.res_r.setdefault(r, []).append(me)
        for w in writes:
            self.res_w[w] = me
            self.res_r[w] = []
        return me

    def emit(self):
        nc = self.nc
        for e in self.engs['pe']:
            e['deps'] = {d for d in e['deps'] if d[0] != 'pe'}
        for eng in COMPUTE:
            for e in self.engs[eng]:
                for (ln, i) in e['deps']:
                    self.lanes[ln][i]['flag'] = True
        for ln, lst in self.lanes.items():
            if lst:
                lst[-1]['flag'] = True
        for ln, lst in self.lanes.items():
            c = 0
            for e in lst:
                if e['flag']:
                    c += 16 if e['dma'] else 1
                e['cnt'] = c
        with ExitStack() as st:
            sems = {}
            for ln, lst in self.lanes.items():
                if lst:
                    sems[ln] = st.enter_context(nc.semaphore("s_" + ln))
            block = st.enter_context(nc.Block())
            totals = {ln: lst[-1]['cnt'] for ln, lst in self.lanes.items() if lst}

            def run(engname, engobj):
                waited = {}
                for e in self.engs[engname]:
                    for (ln, i) in sorted(e['deps']):
                        v = self.lanes[ln][i]['cnt']
                        if waited.get(ln, 0) < v:
                            engobj.wait_ge(sems[ln], v)
                            waited[ln] = v
                    ins = e['fn'](engobj)
                    if e['flag']:
                        ins.then_inc(sems[e['lane']], 16 if e['dma'] else 1)
                if engname == 'sp':
                    for ln, v in totals.items():
                        if waited.get(ln, 0) < v:
                            engobj.wait_ge(sems[ln], v)

            if self.engs['pe']:
                block.tensor(lambda eng: run('pe', eng))
            if self.engs['act']:
                block.scalar(lambda eng: run('act', eng))
            if self.engs['dve']:
                block.vector(lambda eng: run('dve', eng))
            if self.engs['pool']:
                block.gpsimd(lambda eng: run('pool', eng))
            block.sync(lambda eng: run('sp', eng))


C_GATTN, C_GFFN, C_GPLE, C_GPO = 0, 16, 32, 48
C_QNS, C_KNS, C_QND, C_KND, C_GSUB = 64, 65, 66, 67, 68
C_SINK = 69
C_LQ1, C_LK1, C_LQ2, C_LK2 = 77, 141, 205, 269
NCST = 336


def build_nc():
    nc = bass.Bass("TRN2", target_bir_lowering=False)
    dt_in = lambda n, s, d=F32: nc.dram_tensor(n, s, d, kind="ExternalInput").ap()
    xT = dt_in("xT", [D, S])
    pT = dt_in("pT", [256, S])
    w_in = dt_in("w_in", [D, 4352])
    w_out = dt_in("w_out", [D, D])
    w_gate = dt_in("w_gate", [D, DFF])
    w_up = dt_in("w_up", [D, DFF])
    w_down = dt_in("w_down", [DFF, D])
    w_pg = dt_in("w_pg", [D, D])
    w_pp = dt_in("w_pp", [256, D])
    cstd = dt_in("cst", [128, NCST])
    tb_swa = dt_in("tb_swa", [128, 4096])
    aug = dt_in("aug", [48, S])
    tb_mask = dt_in("tb_mask", [128, 128])
    outT = nc.dram_tensor("outT", [D, S], F32, kind="ExternalOutput").ap()
    hbuf = nc.dram_tensor("hbuf", [D, S], F32, kind="Internal").ap()
    ybuf = nc.dram_tensor("ybuf", [D, S], BF16, kind="Internal").ap()
    ubuf = nc.dram_tensor("ubuf", [D, S], BF16, kind="Internal").ap()

    ARENA_B = 194 * 1024

    with ExitStack() as st:
        T = lambda name, shape, dt: st.enter_context(nc.sbuf_tensor(name, shape, dt))
        arena = T("arena", [128, ARENA_B // 2], BF16)
        cst = T("cstt", [128, NCST], F32)
        drv = T("drv", [128, 32], F32)
        ones_f = T("ones_f", [128, 128], F32)
        blk_f = T("blk_f", [128, 128], F32)
        ones_b = T("ones_b", [128, 128], BF16)
        rstd_all = T("rstd_all", [128, S], F32)
        ps = [st.enter_context(nc.psum_tensor("ps%d" % i, [128, 512], F32)) for i in range(8)]
        p = Prog(nc)
        PSR = ['ps%d' % i for i in range(8)]

        class Arena:
            def __init__(self):
                self.off = 0

            def reset(self):
                self.off = 0

            def alloc(self, shape, dt):
                esz = 2 if dt == BF16 else 4
                n = int(np.prod(shape[1:]))
                nb = n * esz
                assert nb % 4 == 0
                v = arena[:, self.off // 2:(self.off + nb) // 2]
                if dt != BF16:
                    v = v.bitcast(dt)
                if len(shape) == 3:
                    v = v.rearrange("p (a b) -> p a b", a=shape[1], b=shape[2])
                elif len(shape) == 4:
                    v = v.rearrange("p (a b c) -> p a b c", a=shape[1], b=shape[2], c=shape[3])
                self.off += nb
                assert self.off <= ARENA_B, ("arena overflow", self.off)
                return v

        A = Arena()

        def MM(out, lhsT, rhs, start, stop, reads, writes):
            p.op('pe', lambda e: e.matmul(out, lhsT=lhsT, rhs=rhs, start=start, stop=stop), reads, writes)

        def ACT(out, in_, func, reads, writes, **kw):
            p.op('act', lambda e: e.activation(out=out, in_=in_, func=func, **kw), reads, writes)

        def TT(out, in0, in1, op, reads, writes, eng='dve'):
            p.op(eng, lambda e: e.tensor_tensor(out=out, in0=in0, in1=in1, op=op), reads, writes)

        def STT(out, in0, scalar, in1, op0, op1, reads, writes):
            p.op('dve', lambda e: e.scalar_tensor_tensor(out=out, in0=in0, scalar=scalar, in1=in1, op0=op0, op1=op1),
                 reads, writes)

        def TS(out, in0, s1, s2, op0, op1, reads, writes):
            p.op('dve', lambda e: e.tensor_scalar(out=out, in0=in0, scalar1=s1, scalar2=s2, op0=op0, op1=op1),
                 reads, writes)

        def DMA(eng, out, in_, reads, writes, lane):
            p.op(eng, lambda e: e.dma_start(out=out, in_=in_), reads, writes, lane=lane)

        class Ring:
            def __init__(self, name, bufs):
                self.name, self.bufs, self.i = name, bufs, 0

            def next(self):
                k = self.i % len(self.bufs)
                self.i += 1
                return self.bufs[k], "%s%d" % (self.name, k), "d_%s%d" % (self.name, k)

        DMA('sp', cst[:], cstd, [], ['cst'], 'd_cst')
        p.op('dve', lambda e: e.memset(ones_f[:], 1.0), [], ['ones_f'])
        p.op('dve', lambda e: e.memset(blk_f[:], 0.0), [], ['blk_f'])
        p.op('dve', lambda e: e.memset(blk_f[0:64, 0:64], 1.0), [], ['blk_f'])
        p.op('dve', lambda e: e.memset(blk_f[64:128, 64:128], 1.0), [], ['blk_f'])
        p.op('dve', lambda e: e.memset(ones_b[:], 1.0), [], ['ones_b'])
        ones_h = [T("ones_h%d" % i, [128, 128], BF16) for i in range(2)]
        for i in range(2):
            p.op('dve', lambda e, i=i: e.memset(ones_h[i][:], 0.0), [], ['ones_b'])
            p.op('dve', lambda e, i=i: e.memset(ones_h[i][:, i * 64:(i + 1) * 64], 1.0), [], ['ones_b'])
        TS(drv[:, 0:1], cst[:, C_QNS:C_QNS + 1], 0.125, None, ALU.mult, ALU.bypass, ['cst'], ['drv'])
        TS(drv[:, 1:2], cst[:, C_QND:C_QND + 1], 0.125, None, ALU.mult, ALU.bypass, ['cst'], ['drv'])
        TS(drv[:, 2:3], cst[:, C_GSUB:C_GSUB + 1], 0.8, None, ALU.mult, ALU.bypass, ['cst'], ['drv'])
        lam_t = T("lam_t", [128, 128], F32)
        TT(lam_t[:, 0:64], cst[:, C_LQ1:C_LQ1 + 64], cst[:, C_LK1:C_LK1 + 64], ALU.mult, ['cst'], ['lam_t'])
        TT(lam_t[:, 64:128], cst[:, C_LQ2:C_LQ2 + 64], cst[:, C_LK2:C_LK2 + 64], ALU.mult, ['cst'], ['lam_t'])
        p.op('dve', lambda e: e.reduce_sum(out=drv[:, 12:13], in_=lam_t[:, 0:64], axis=mybir.AxisListType.X),
             ['lam_t'], ['drv'])
        p.op('dve', lambda e: e.reduce_sum(out=drv[:, 13:14], in_=lam_t[:, 64:128], axis=mybir.AxisListType.X),
             ['lam_t', 'drv'], ['drv'])
        ACT(drv[:, 14:16], drv[:, 12:14], AF.Exp, ['drv'], ['drv'])
        ACT(drv[:, 4:12], cst[:, C_SINK:C_SINK + 8], AF.Exp, ['cst', 'drv'], ['drv'])
        TT(drv[:, 3:4], drv[:, 15:16], drv[:, 14:15], ALU.subtract, ['drv'], ['drv'])
        TS(drv[:, 3:4], drv[:, 3:4], -0.2, None, ALU.add, ALU.bypass, ['drv'], ['drv'])

        def rstd_from(ps_ap, out_ap, tmp_ap, n, reads, tmp_res, out_res):
            ACT(tmp_ap, ps_ap, AF.Ln, reads, [tmp_res], scale=1.0 / n, bias=EPS)
            ACT(out_ap, tmp_ap, AF.Exp, [tmp_res], [out_res], scale=-0.5)

        A.reset()
        uT = A.alloc([128, 16, S], BF16)
        markB = A.off
        xin = Ring('xin', [A.alloc([128, 16, 512], F32) for _ in range(2)])
        sqr = Ring('sqa', [A.alloc([128, 512], F32) for _ in range(4)])
        lnb = Ring('lna', [A.alloc([128, 512], F32) for _ in range(2)])
        rsb = Ring('rsa', [A.alloc([128, 512], F32) for _ in range(2)])
        xv = xT.rearrange("(kc p) s -> p kc s", p=128)
        for tc in range(4):
            xt, xr, xl = xin.next()
            DMA('sp', xt, xv[:, :, tc * 512:(tc + 1) * 512], [], [xr], xl)
            b = tc % 2
            for k in range(16):
                sq, sr, _ = sqr.next()
                ACT(sq, xt[:, k, :], AF.Square, [xr], [sr])
                MM(ps[b][:], ones_f[:], sq, k == 0, k == 15, ['ones_f', sr], [PSR[b]])
            lt, lr, _ = lnb.next()
            rt, rr, _ = rsb.next()
            rstd_from(ps[b][:], rt, lt, D, [PSR[b]], lr, rr)
            for k in range(16):
                STT(uT[:, k, tc * 512:(tc + 1) * 512], xt[:, k, :], cst[:, C_GATTN + k:C_GATTN + k + 1], rt,
                    ALU.mult, ALU.mult, [xr, rr, 'cst'], [('uT', k, tc)])
        UT_ALL = [('uT', k, tc) for k in range(16) for tc in range(4)]

        p.barrier()
        A.off = markB
        tbl = A.alloc([128, 5120], F32)
        unit_q = A.alloc([128, 4, S], BF16)
        kp = [A.alloc([128, S], BF16) for _ in range(2)]
        unit_v = A.alloc([128, 16, 128], BF16)
        unit_v2 = A.alloc([128, 16, 128], BF16)
        p.op('pool', lambda e: e.memset(kp[0][64:128, :], 0.0), [], ['unit_k'])
        p.op('pool', lambda e: e.memset(kp[1][0:64, :], 0.0), [], ['unit_k'])
        p.op('pool', lambda e: e.memset(unit_v[:, :, 64:128], 0.0), [], ['unit_v'])
        p.op('pool', lambda e: e.memset(unit_v2[:, :, 0:64], 0.0), [], ['unit_v'])
        wrB = Ring('wrB', [A.alloc([128, 16, 256], BF16) for _ in range(2)])
        wk32 = Ring('wk32_', [A.alloc([128, 512], F32) for _ in range(10)])
        wk16 = Ring('wk16_', [A.alloc([128, 512], BF16) for _ in range(4)])
        ystage = Ring('yst', [A.alloc([128, 4, S], BF16)])
        fin32 = Ring('fin32_', [A.alloc([128, 512], F32) for _ in range(10)])
        sctr = [0]

        def run_pipeline(steps, front, back, L):
            q = []
            deferred = []

            def tick():
                for d in deferred:
                    d[0] -= 1
                ready = [d for d in deferred if d[0] <= 0]
                deferred[:] = [d for d in deferred if d[0] > 0]
                for d in ready:
                    d[1]()

            for stp in steps:
                q.append((stp, front(stp)))
                if len(q) > L:
                    s0, i0 = q.pop(0)
                    for (dl, fn) in back(s0, i0):
                        deferred.append([dl, fn])
                tick()
            while q:
                s0, i0 = q.pop(0)
                for (dl, fn) in back(s0, i0):
                    deferred.append([dl, fn])
                tick()
            while deferred:
                tick()
        w_in_v = w_in.rearrange("(kc p) n -> p kc n", p=128)

        zsets = [(0, 1), (2, 3), (4, 5)]
        zctr = [0]
        ssb = [0]
        pending = []

        def flush_one():
            (zb, dsts, dres, gcol, blk) = pending.pop(0)
            for j in range(2):
                sq, sr = zb[2 + j]
                sb = 6 + (ssb[0] % 2)
                ssb[0] += 1
                MM(ps[sb][:], blk[:], sq, True, True, ['blk_f', 'ones_f', sr], [PSR[sb]])
                lt, lr, _ = wk32.next()
                rt, rr, _ = wk32.next()
                rstd_from(ps[sb][:], rt, lt, 64 if blk is blk_f else 128, [PSR[sb]], lr, rr)
                for (r0, r1, dap) in dsts[j]:
                    STT(dap, ps[zb[j]][r0:r1, :], gcol[r0:r1, :], rt[r0:r1, :], ALU.mult, ALU.mult,
                        [PSR[zb[j]], rr, 'drv', 'cst'], [dres])

        def proj_fm_norm(wt, wres, c0, gcol, dst_fn, dres):
            for hc in range(2):
                zb = zsets[zctr[0] % 3]
                zctr[0] += 1
                for k in range(16):
                    for j in range(2):
                        tcx = hc * 2 + j
                        MM(ps[zb[j]][:], wt[:, k, c0:c0 + 128], uT[:, k, tcx * 512:(tcx + 1) * 512],
                           k == 0, k == 15, [wres, ('uT', k, tcx)], [PSR[zb[j]]])
                sqs = []
                for j in range(2):
                    sq, sr, _ = wk32.next()
                    ACT(sq, ps[zb[j]][:], AF.Square, [PSR[zb[j]]], [sr])
                    sqs.append((sq, sr))
                if pending:
                    flush_one()
                pending.append(((zb[0], zb[1], sqs[0], sqs[1]),
                                [dst_fn(hc * 2), dst_fn(hc * 2 + 1)], dres, gcol, blk_f))

        def flush_all():
            while pending:
                flush_one()

        def proj_tm(wt, wres, c0, ncol, dst, dres):
            per = 512 // ncol
            nb = 16 // per
            for tb in range(16):
                bk = tb // per
                co = (tb % per) * ncol
                for k in range(16):
                    MM(ps[bk][:, co:co + ncol], uT[:, k, tb * 128:(tb + 1) * 128], wt[:, k, c0:c0 + ncol],
                       k == 0, k == 15, [wres, ('uT', k, tb // 4)], [PSR[bk]])
            ei = 0
            for bk in range(nb):
                src = ps[bk][:].rearrange("p (a b) -> p a b", a=per, b=ncol)
                for (dt_, dc, sc, wd) in dst:
                    dv = dt_[:, bk * per:(bk + 1) * per, dc:dc + wd]
                    sv = src[:, :, sc:sc + wd]
                    if ei % 2 == 0:
                        p.op('dve', lambda e, dv=dv, sv=sv: e.tensor_copy(out=dv, in_=sv), [PSR[bk]], [dres])
                    else:
                        ACT(dv, sv, AF.Copy, [PSR[bk]], [dres])
                    ei += 1

        def load_w(ring, dst_cols, src_ap_list):
            wt, wres, wl = ring.next()
            for (c0, c1), src in zip(dst_cols, src_ap_list):
                DMA('pool', wt[:, :, c0:c1], src, [], [wres], wl)
            return wt, wres

        DMA('sp', tbl[:, 0:4096], tb_swa, [], ['tbl'], 'd_tbl')
        tblv = tbl[:, 0:4096].rearrange("p (a b) -> p a b", a=8, b=512)
        for g in range(2):
            yt, yr, yl = ystage.next()
            for half in range(2):
                c = g * 512 + half * 256
                wt, wres = load_w(wrB, [(0, 256)], [w_in_v[:, :, c:c + 256]])
                for cc in range(2):
                    qi = half * 2 + cc
                    proj_fm_norm(wt, wres, cc * 128, drv[:, 0:1],
                                 lambda tc, qi=qi: [(0, 128, unit_q[:, qi, tc * 512:(tc + 1) * 512])], 'unit_q')
            kc0 = 1024 + g * 64
            vc0 = 1152 + g * 64
            wt, wres = load_w(wrB, [(0, 64), (64, 128), (128, 192)],
                              [w_in_v[:, :, kc0:kc0 + 64], w_in_v[:, :, kc0:kc0 + 64], w_in_v[:, :, vc0:vc0 + 64]])
            proj_fm_norm(wt, wres, 0, cst[:, C_KNS:C_KNS + 1],
                         lambda tc: [(0, 64, kp[0][0:64, tc * 512:(tc + 1) * 512]), (64, 128, kp[1][64:128, tc * 512:(tc + 1) * 512])], 'unit_k')
            flush_all()
            proj_tm(wt, wres, 128, 64, [(unit_v, 0, 0, 64), (unit_v2, 64, 0, 64)], 'unit_v')
            esb = drv[:, 4 + g * 4:8 + g * 4]
            es_bc = bass.AP(esb.tensor, esb.offset, [list(esb.ap[0]), [1, 4], [0, 128]])
            steps = []
            for n in range(16):
                kbs = [n - 1, n] if n > 0 else [n]
                for par in range(2):
                    for kb in kbs:
                        steps.append((n, par, kb, par == 0 and kb == kbs[0], par == 1 and kb == kbs[-1],
                                      par == 1 and kb == kbs[-1]))

            def swa_front(stp, g=g):
                (n, par, kb, first, last, fin) = stp
                kind = 0 if kb == n else 1
                sb = sctr[0] % 4
                sctr[0] += 1
                MM(ps[sb][:], kp[par][:, kb * 128:(kb + 1) * 128],
                   unit_q[:, :, n * 128:(n + 1) * 128], True, True,
                   ['unit_k', 'unit_q'], [PSR[sb]])
                tt, tr, _ = wk32.next()
                TT(tt, ps[sb][:], tblv[:, kind * 4 + g * 2 + par, :], ALU.add, [PSR[sb], 'tbl'], [tr])
                et, er, _ = wk16.next()
                ACT(et, tt, AF.Exp, [tr], [er])
                return (et, er)

            def swa_back(stp, info, yt=yt, yr=yr):
                (n, par, kb, first, last, fin) = stp
                (et, er) = info
                pvb = 4 + (n % 2)
                dnb = 6 + (n % 2)
                MM(ps[pvb][:], (unit_v if par == 0 else unit_v2)[:, kb, :], et, first, last,
                   ['unit_v', er], [PSR[pvb]])
                MM(ps[dnb][:], ones_h[par][:], et, first, last,
                   ['ones_b', er], [PSR[dnb]])
                if not fin:
                    return []
                dd, dr, _ = fin32.next()
                lt, lr, _ = fin32.next()
                rt, rr, _ = fin32.next()

                def stage_a():
                    TT(dd.rearrange("p (a b) -> p a b", a=4, b=128),
                       ps[dnb][:].rearrange("p (a b) -> p a b", a=4, b=128),
                       es_bc, ALU.add, [PSR[dnb], 'drv'], [dr])
                    ACT(lt, dd, AF.Ln, [dr], [lr])
                    ACT(rt, lt, AF.Exp, [lr], [rr], scale=-1.0)

                def stage_b():
                    TT(yt[:, :, n * 128:(n + 1) * 128], ps[pvb][:].rearrange("p (a b) -> p a b", a=4, b=128),
                       rt.rearrange("p (a b) -> p a b", a=4, b=128), ALU.mult, [PSR[pvb], rr], [yr])
                return [(1, stage_a), (3, stage_b)]

            run_pipeline(steps, swa_front, swa_back, 3)
            for i in range(4):
                ch = g * 4 + i
                DMA('sp', ybuf[ch * 128:(ch + 1) * 128, :], yt[:, i, :], [yr], [('ybuf', ch)], yl)

        maskt = tbl[:, 0:128]
        DMA('sp', maskt, tb_mask, [], ['tbl'], 'd_tbl')
        qp = [unit_q[:, 0, :], unit_q[:, 1, :]]
        p.op('pool', lambda e: e.memset(unit_q[:, 0:2, :], 0.0), [], ['unit_q'])
        aug_v = aug.rearrange("(h a r) s -> h a r s", h=8, a=2, r=3)
        for h in range(8):
            slope = 2.0 ** (-(h + 1))
            yt, yr, yl = ystage.next()
            qc0 = 1280 + h * 128
            kc0 = 2304 + h * 128
            vc0 = 3328 + h * 128
            DMA('pool', kp[0][64:67, :], aug_v[h, 0], [], ['aug'], 'd_aug')
            DMA('pool', kp[1][0:3, :], aug_v[h, 0], [], ['aug'], 'd_aug')
            DMA('pool', qp[0][64:67, :], aug_v[h, 1], [], ['aug'], 'd_aug')
            DMA('pool', qp[1][0:3, :], aug_v[h, 1], [], ['aug'], 'd_aug')
            wt, wres = load_w(wrB, [(0, 128), (128, 256)], [w_in_v[:, :, qc0:qc0 + 128], w_in_v[:, :, kc0:kc0 + 128]])
            proj_fm_norm(wt, wres, 0, drv[:, 1:2],
                         lambda tc: [(0, 64, qp[0][0:64, tc * 512:(tc + 1) * 512]),
                                     (64, 128, qp[1][64:128, tc * 512:(tc + 1) * 512])], 'unit_q')
            proj_fm_norm(wt, wres, 128, cst[:, C_KND:C_KND + 1],
                         lambda tc: [(0, 64, kp[0][0:64, tc * 512:(tc + 1) * 512]),
                                     (64, 128, kp[1][64:128, tc * 512:(tc + 1) * 512])], 'unit_k')
            flush_all()
            if h % 2 == 0:
                wt, wres = load_w(wrB, [(0, 256)], [w_in_v[:, :, vc0:vc0 + 256]])
                proj_tm(wt, wres, 0, 256, [(unit_v, 0, 0, 128), (unit_v2, 0, 128, 128)], 'unit_v')
            vd = unit_v if h % 2 == 0 else unit_v2
            steps = []
            for c in range(4):
                for m in range(2):
                    pi = c * 2 + m
                    for j in range(4 * c + 4):
                        steps.append((c, m, j, 4 + 2 * (pi % 2), 5 + 2 * (pi % 2)))
            ats = {}

            def diff_front(stp, h=h, slope=slope):
                (c, m, j, pvb, dnb) = stp
                diag = j >= 4 * c
                jk = j - 4 * c if diag else 0
                N = 512 - jk * 128
                off = 512 - N
                q0 = c * 512 + off
                sb = sctr[0] % 4
                sctr[0] += 1
                MM(ps[sb][:, 0:N], kp[m][:, j * 128:(j + 1) * 128],
                   qp[m][:, q0:q0 + N], True, True, ['unit_k', 'unit_q', 'aug'], [PSR[sb]])
                cb = float(-slope * 128.0 * (4 * c - j))
                et, er, _ = wk16.next()
                if diag:
                    tt, tr, _ = wk32.next()
                    TT(tt[:, 0:128], ps[sb][:, 0:128], maskt, ALU.add, [PSR[sb], 'tbl'], [tr])
                    ACT(et[:, 0:128], tt[:, 0:128], AF.Exp, [tr], [er], bias=cb)
                    if N > 128:
                        ACT(et[:, 128:N], ps[sb][:, 128:N], AF.Exp, [PSR[sb]], [er], bias=cb)
                else:
                    ACT(et[:, 0:N], ps[sb][:, 0:N], AF.Exp, [PSR[sb]], [er], bias=cb)
                return (et, er, N, off)

            def diff_back(stp, info, yt=yt, yr=yr, vd=vd):
                (c, m, j, pvb, dnb) = stp
                (et, er, N, off) = info
                last_j = 4 * c + 3
                MM(ps[pvb][:, off:512], vd[:, j, :], et[:, 0:N], j == 0, j == last_j, ['unit_v', er], [PSR[pvb]])
                MM(ps[dnb][:, off:512], ones_b[:], et[:, 0:N], j == 0, j == last_j, ['ones_b', er], [PSR[dnb]])
                if j != last_j:
                    return []
                lt, lr, _ = fin32.next()
                rt, rr, _ = fin32.next()
                at, ar, _ = fin32.next()
                ats[(c, m)] = (at, ar)

                def mk_rc(i):
                    def f():
                        p.op('dve', lambda e: e.reciprocal(out=rt[:, i * 128:(i + 1) * 128],
                                                           in_=ps[dnb][:, i * 128:(i + 1) * 128]), [PSR[dnb]], [rr])
                    return f

                def stage_b():
                    TT(at, ps[pvb][:], rt, ALU.mult, [PSR[pvb], rr], [ar])
                outl = [(1, mk_rc(0)), (1, mk_rc(1)), (2, mk_rc(2)), (3, mk_rc(3)), (4, stage_b)]
                if m == 1:
                    od, odr, _ = fin32.next()
                    sq, sr, _ = fin32.next()
                    lt2, lr2, _ = fin32.next()
                    rt2, rr2, _ = fin32.next()

                    def stage_c():
                        a0, a0r = ats[(c, 0)]
                        a1, a1r = ats[(c, 1)]
                        STT(od, a1, drv[:, 3:4], a0, ALU.mult, ALU.add, [a0r, a1r, 'drv'], [odr])
                        ACT(sq, od, AF.Square, [odr], [sr])

                    def stage_d():
                        sb = sctr[0] % 4
                        sctr[0] += 1
                        MM(ps[sb][:], ones_f[:], sq, True, True, ['ones_f', sr], [PSR[sb]])
                        rstd_from(ps[sb][:], rt2, lt2, 128, [PSR[sb]], lr2, rr2)

                    def stage_e():
                        STT(yt[:, 0, c * 512:(c + 1) * 512], od, drv[:, 2:3], rt2, ALU.mult, ALU.mult,
                            [odr, rr2, 'drv'], [yr])
                    outl += [(5, stage_c), (6, stage_d), (8, stage_e)]
                return outl

            run_pipeline(steps, diff_front, diff_back, 3)
            ch = 8 + h
            DMA('sp', ybuf[ch * 128:(ch + 1) * 128, :], yt[:, 0, :], [yr], [('ybuf', ch)], yl)

        p.barrier()
        A.reset()
        yT = A.alloc([128, 16, S], BF16)
        xc = Ring('xc', [A.alloc([128, S], F32) for _ in range(2)])
        hc_r = Ring('hc', [A.alloc([128, S], F32) for _ in range(2)])
        sq_r = Ring('sqc', [A.alloc([128, S], F32) for _ in range(2)])
        accsq = A.alloc([128, S], F32)
        wrC = Ring('wrC', [A.alloc([128, 16, 512], BF16) for _ in range(2)])
        tmpc = A.alloc([128, S], F32)
        ust_r = Ring('ust', [A.alloc([128, S], BF16) for _ in range(2)])
        yv = ybuf.rearrange("(kc p) s -> p kc s", p=128)
        for gk in range(4):
            DMA('sp', yT[:, gk * 4:(gk + 1) * 4, :], yv[:, gk * 4:(gk + 1) * 4, :],
                [('ybuf', k) for k in range(gk * 4, gk * 4 + 4)], [('yT', gk)], 'd_yT%d' % gk)
        w_out_v = w_out.rearrange("(kc p) n -> p kc n", p=128)
        def ld_x(n):
            xt, xr, xl = xc.next()
            DMA('sp', xt, xT[n * 128:(n + 1) * 128, :], [], [xr], xl)
            return xt, xr
        xq = [ld_x(0)]
        wq = [load_w(wrC, [(0, 512)], [w_out_v[:, :, 0:512]])]
        for sn in range(4):
            if sn + 1 < 4:
                wq.append(load_w(wrC, [(0, 512)], [w_out_v[:, :, (sn + 1) * 512:(sn + 2) * 512]]))
            wt, wres = wq.pop(0)
            for n4 in range(4):
                n = sn * 4 + n4
                banks = (0, 1, 2, 3) if n % 2 == 0 else (4, 5, 6, 7)
                if n + 1 < 16:
                    xq.append(ld_x(n + 1))
                for k in range(16):
                    for tc in range(4):
                        MM(ps[banks[tc]][:], wt[:, k, n4 * 128:(n4 + 1) * 128], yT[:, k, tc * 512:(tc + 1) * 512],
                           k == 0, k == 15, [wres, ('yT', k // 4)], [PSR[banks[tc]]])
                xt, xr = xq.pop(0)
                ht, hr, hl = hc_r.next()
                for tc in range(4):
                    TT(ht[:, tc * 512:(tc + 1) * 512], ps[banks[tc]][:], xt[:, tc * 512:(tc + 1) * 512], ALU.add,
                       [PSR[banks[tc]], xr], [hr])
                DMA('sp', hbuf[n * 128:(n + 1) * 128, :], ht, [hr], [('hbuf', n, 0), ('hbuf', n, 1)], hl)
                ut, utr, utl = ust_r.next()
                ACT(ut, ht, AF.Copy, [hr, 'cst'], [utr], scale=cst[:, C_GFFN + n:C_GFFN + n + 1])
                DMA('sp', ubuf[n * 128:(n + 1) * 128, :], ut, [utr], [('ubuf', n)], utl)
                if n == 0:
                    ACT(accsq, ht, AF.Square, [hr], ['accsq'])
                else:
                    sq, sr, _ = sq_r.next()
                    ACT(sq, ht, AF.Square, [hr], [sr])
                    TT(accsq, accsq, sq, ALU.add, [sr], ['accsq'], eng='pool')

        def finish_rstd(acc_ap, acc_res, ntok, tmp_ap, out_ap, out_res, banks):
            for i in range(ntok // 512):
                b = banks[i]
                MM(ps[b][:], ones_f[:], acc_ap[:, i * 512:(i + 1) * 512], True, True, ['ones_f', acc_res], [PSR[b]])
                rstd_from(ps[b][:], out_ap[:, i * 512:(i + 1) * 512], tmp_ap[:, i * 512:(i + 1) * 512], D,
                          [PSR[b]], 'tmp_rs', out_res)

        finish_rstd(accsq, 'accsq', S, tmpc, rstd_all, 'rstd_all', (0, 1, 2, 3))

        w_gate_v = w_gate.rearrange("(kc p) n -> p kc n", p=128)
        w_up_v = w_up.rearrange("(kc p) n -> p kc n", p=128)
        w_down_v = w_down.rearrange("(f p) n -> p f n", p=128)
        w_pg_v = w_pg.rearrange("(kc p) n -> p kc n", p=128)
        w_pp_v = w_pp.rearrange("(kc p) n -> p kc n", p=128)
        pT_v = pT.rearrange("(kc p) s -> p kc s", p=128)
        HT = 1024

        def make_u(u, hf, gc0, rstd_ap, rstd_res, hin):
            for k in range(16):
                it, ir, il = hin.next()
                DMA('sp', it, hbuf[k * 128:(k + 1) * 128, hf * HT:(hf + 1) * HT], [('hbuf', k, hf)], [ir], il)
                STT(u[:, k, :], it, cst[:, gc0 + k:gc0 + k + 1], rstd_ap, ALU.mult, ALU.mult,
                    [ir, rstd_res, 'cst'], [('u', k)])

        for hf in range(2):
            p.barrier()
            A.reset()
            u = A.alloc([128, 16, HT], BF16)
            aT = A.alloc([128, NF, HT], BF16)
            markW = A.off
            ubv = ubuf.rearrange("(kc p) s -> p kc s", p=128)
            for gk in range(4):
                DMA('sp', u[:, gk * 4:(gk + 1) * 4, :], ubv[:, gk * 4:(gk + 1) * 4, hf * HT:(hf + 1) * HT],
                    [('ubuf', k) for k in range(gk * 4, gk * 4 + 4)], [('u', k) for k in range(gk * 4, gk * 4 + 4)],
                    'd_u%d' % gk)
            rs2 = rstd_all[:, hf * HT:(hf + 1) * HT]
            wrg = Ring('wrg', [A.alloc([128, 16, 256], BF16) for _ in range(2)])
            wru = Ring('wru', [A.alloc([128, 16, 256], BF16) for _ in range(2)])
            sil = Ring('sil', [A.alloc([128, 512], F32) for _ in range(9)])
            for fp in range(NF // 2):
                wg, wgr = load_w(wrg, [(0, 256)], [w_gate_v[:, :, fp * 256:(fp + 1) * 256]])
                wu, wur = load_w(wru, [(0, 256)], [w_up_v[:, :, fp * 256:(fp + 1) * 256]])
                for f2 in range(2):
                    f = fp * 2 + f2
                    gb, ub = ((0, 1), (2, 3)) if f % 2 == 0 else ((4, 5), (6, 7))
                    for k in range(16):
                        for tc in range(2):
                            MM(ps[gb[tc]][:], wg[:, k, f2 * 128:(f2 + 1) * 128], u[:, k, tc * 512:(tc + 1) * 512],
                               k == 0, k == 15, [wgr, ('u', k)], [PSR[gb[tc]]])
                    for k in range(16):
                        for tc in range(2):
                            MM(ps[ub[tc]][:], wu[:, k, f2 * 128:(f2 + 1) * 128], u[:, k, tc * 512:(tc + 1) * 512],
                               k == 0, k == 15, [wur, ('u', k)], [PSR[ub[tc]]])
                    for tc in range(2):
                        rsl = rs2[:, tc * 512:(tc + 1) * 512]
                        t1, t1r, _ = sil.next()
                        TT(t1, ps[gb[tc]][:], rsl, ALU.mult, [PSR[gb[tc]], 'rstd_all'], [t1r])
                        stt, sr, _ = sil.next()
                        ACT(stt, t1, AF.Silu, [t1r], [sr])
                        t2, t2r, _ = sil.next()
                        TT(t2, ps[ub[tc]][:], rsl, ALU.mult, [PSR[ub[tc]], 'rstd_all'], [t2r])
                        TT(aT[:, f, tc * 512:(tc + 1) * 512], stt, t2, ALU.mult, [sr, t2r], [('aT', f)])
            p.barrier()
            A.off = markW
            rstd2 = A.alloc([128, HT], F32)
            h1c = Ring('h1c', [A.alloc([128, HT], F32) for _ in range(2)])
            h2c = Ring('h2c', [A.alloc([128, HT], F32) for _ in range(2)])
            sqd = Ring('sqd', [A.alloc([128, HT], F32) for _ in range(1)])
            acc2 = A.alloc([128, HT], F32)
            wrd = Ring('wrd', [A.alloc([128, NF, 256], BF16) for _ in range(2)])
            tmp2 = sqd.bufs[0]
            def ld_wd(ng):
                wt_, wres_, wl_ = wrd.next()
                DMA('pool', wt_, w_down_v[:, :, ng * 256:(ng + 1) * 256], [], [wres_], wl_)
                return wt_, wres_

            def ld_h1(n):
                it_, ir_, il_ = h1c.next()
                DMA('sp', it_, hbuf[n * 128:(n + 1) * 128, hf * HT:(hf + 1) * HT], [('hbuf', n, hf)], [ir_], il_)
                return it_, ir_
            hq = [ld_h1(0)]
            wq = [ld_wd(0)]
            for ng in range(8):
                if ng + 1 < 8:
                    wq.append(ld_wd(ng + 1))
                wd, wdr0 = wq.pop(0)
                for n2 in range(2):
                    n = ng * 2 + n2
                    wdr = wdr0
                    if n + 1 < 16:
                        hq.append(ld_h1(n + 1))
                    bb = ((0, 1), (2, 3), (4, 5), (6, 7))[n % 4]
                    for f in range(NF):
                        for tc in range(2):
                            MM(ps[bb[tc]][:], wd[:, f, n2 * 128:(n2 + 1) * 128], aT[:, f, tc * 512:(tc + 1) * 512],
                               f == 0, f == NF - 1, [wdr, ('aT', f)], [PSR[bb[tc]]])
                    it, ir = hq.pop(0)
                    ot, orr, ol = h2c.next()
                    for tc in range(2):
                        TT(ot[:, tc * 512:(tc + 1) * 512], ps[bb[tc]][:], it[:, tc * 512:(tc + 1) * 512], ALU.add,
                           [PSR[bb[tc]], ir], [orr])
                    DMA('sp', hbuf[n * 128:(n + 1) * 128, hf * HT:(hf + 1) * HT], ot, [orr], [('hbuf', n, hf)], ol)
                    ACT(u[:, n, :], ot, AF.Copy, [orr, 'cst'], [('u', n)], scale=cst[:, C_GPLE + n:C_GPLE + n + 1])
                    if n == 0:
                        ACT(acc2, ot, AF.Square, [orr], ['acc2'])
                    else:
                        sq, sr, _ = sqd.next()
                        ACT(sq, ot, AF.Square, [orr], [sr])
                        TT(acc2, acc2, sq, ALU.add, [sr], ['acc2'], eng='pool')
            finish_rstd(acc2, 'acc2', HT, tmp2, rstd2, 'rstd2', (0, 1))
            p.barrier()
            A.off = markW
            A.alloc([128, HT], F32)
            pTb = A.alloc([128, 2, HT], BF16)
            wpp = A.alloc([128, 2, D], BF16)
            wrp = Ring('wrp', [A.alloc([128, 16, 256], BF16) for _ in range(2)])
            acc3 = A.alloc([128, HT], F32)
            rstd3 = A.alloc([128, HT], F32)
            tmp3 = A.alloc([128, HT], F32)
            sq3 = Ring('sq3_', [A.alloc([128, 512], F32) for _ in range(2)])
            sg_r = Ring('sg', [A.alloc([128, 512], F32) for _ in range(2)])
            t_r = Ring('tpl', [A.alloc([128, 512], F32) for _ in range(2)])
            h3c = Ring('h3c', [A.alloc([128, HT], F32) for _ in range(2)])
            o3c = Ring('o3c', [A.alloc([128, HT], F32) for _ in range(2)])
            DMA('pool', pTb, pT_v[:, :, hf * HT:(hf + 1) * HT], [], ['pTb'], 'd_pTb')
            DMA('pool', wpp, w_pp_v, [], ['wpp'], 'd_wpp')
            for n in range(16):
                for tc in range(2):
                    b = (n * 2 + tc) % 4
                    for kc in range(2):
                        MM(ps[b][:], wpp[:, kc, n * 128:(n + 1) * 128], pTb[:, kc, tc * 512:(tc + 1) * 512],
                           kc == 0, kc == 1, ['wpp', 'pTb'], [PSR[b]])
                    if n == 0:
                        ACT(acc3[:, tc * 512:(tc + 1) * 512], ps[b][:], AF.Square, [PSR[b]], [('acc3', tc)])
                    else:
                        sq, sr, _ = sq3.next()
                        ACT(sq, ps[b][:], AF.Square, [PSR[b]], [sr])
                        TT(acc3[:, tc * 512:(tc + 1) * 512], acc3[:, tc * 512:(tc + 1) * 512], sq, ALU.add,
                           [sr], [('acc3', tc)], eng=('dve' if tc == 0 else 'pool'))
            for i in range(2):
                b = 4 + i
                MM(ps[b][:], ones_f[:], acc3[:, i * 512:(i + 1) * 512], True, True, ['ones_f', ('acc3', i)], [PSR[b]])
                rstd_from(ps[b][:], rstd3[:, i * 512:(i + 1) * 512], tmp3[:, i * 512:(i + 1) * 512], D,
                          [PSR[b]], 'tmp3', ('rstd3', i))
            def ld_h3(n):
                it_, ir_, il_ = h3c.next()
                DMA('sp', it_, hbuf[n * 128:(n + 1) * 128, hf * HT:(hf + 1) * HT], [('hbuf', n, hf)], [ir_], il_)
                return it_, ir_
            hq = [ld_h3(0)]
            wq = [load_w(wrp, [(0, 256)], [w_pg_v[:, :, 0:256]])]
            for sn in range(8):
                if sn + 1 < 8:
                    wq.append(load_w(wrp, [(0, 256)], [w_pg_v[:, :, (sn + 1) * 256:(sn + 2) * 256]]))
                wt, wres = wq.pop(0)
                for n2 in range(2):
                    n = sn * 2 + n2
                    gb, eb = ((0, 1), (2, 3)) if n % 2 == 0 else ((4, 5), (6, 7))
                    if n + 1 < 16:
                        hq.append(ld_h3(n + 1))
                    for k in range(16):
                        for tc in range(2):
                            MM(ps[gb[tc]][:], wt[:, k, n2 * 128:(n2 + 1) * 128], u[:, k, tc * 512:(tc + 1) * 512],
                               k == 0, k == 15, [wres, ('u', k)], [PSR[gb[tc]]])
                    for tc in range(2):
                        for kc in range(2):
                            MM(ps[eb[tc]][:], wpp[:, kc, n * 128:(n + 1) * 128], pTb[:, kc, tc * 512:(tc + 1) * 512],
                               kc == 0, kc == 1, ['wpp', 'pTb'], [PSR[eb[tc]]])
                    it, ir = hq.pop(0)
                    ot, orr, ol = o3c.next()
                    for tc in range(2):
                        sg, sgr, _ = sg_r.next()
                        TT(sg, ps[gb[tc]][:], rstd2[:, tc * 512:(tc + 1) * 512], ALU.mult, [PSR[gb[tc]], 'rstd2'], [sgr])
                        ACT(sg, sg, AF.Sigmoid, [sgr], [sgr])
                        tt, tr, _ = t_r.next()
                        STT(tt, ps[eb[tc]][:], cst[:, C_GPO + n:C_GPO + n + 1], rstd3[:, tc * 512:(tc + 1) * 512],
                            ALU.mult, ALU.mult, [PSR[eb[tc]], ('rstd3', tc), 'cst'], [tr])
                        TT(tt, tt, sg, ALU.mult, [sgr], [tr])
                        TT(ot[:, tc * 512:(tc + 1) * 512], tt, it[:, tc * 512:(tc + 1) * 512], ALU.add,
                           [tr, ir], [orr], eng='pool')
                    DMA('sp', outT[n * 128:(n + 1) * 128, hf * HT:(hf + 1) * HT], ot, [orr], [('out', n, hf)], ol)

        p.emit()
    return nc


def _tables():
    k = np.arange(128, dtype=np.float32)[:, None]
    q = np.arange(128, dtype=np.float32)[None, :]
    sl_swa = np.asarray([2.0 ** (-8.0 * (h + 1) / 16) for h in range(16)], dtype=np.float32)
    sl_d = np.asarray([2.0 ** (-8.0 * (h + 1) / 8) for h in range(8)], dtype=np.float32)
    tb_swa = np.zeros((128, 2, 2, 2, 4, 128), np.float32)
    for kind in range(2):
        dist = (q - k) if kind == 0 else (q - k + 128.0)
        valid = (dist >= 0) & (dist < 128)
        for g in range(2):
            for par in range(2):
                for i in range(4):
                    H = 8 * g + 2 * i + par
                    tb_swa[:, kind, g, par, i, :] = np.where(valid, -sl_swa[H] * dist, NEG).astype(np.float32)
    aug = np.zeros((8, 2, 3, S), np.float32)
    t = np.arange(S)
    for h in range(8):
        aug[h, 0, 0] = sl_d[h] * (t % 128)
        aug[h, 0, 1] = 1.0
        aug[h, 0, 2] = 1.0
        aug[h, 1, 0] = 1.0
        aug[h, 1, 1] = -sl_d[h] * (t % 128)
        aug[h, 1, 2] = -sl_d[h] * 128.0 * ((t // 128) % 4)
    tb_mask = np.where(q - k >= 0, 0.0, NEG).astype(np.float32)
    return (tb_swa.reshape(128, 4096), aug.reshape(48, S), tb_mask)


def _pack_consts(g_attn, g_ffn, g_ple, g_ple_out, qn_swa, kn_swa, qn_diff, kn_diff, g_sub, sinks, lq1, lk1, lq2, lk2):
    c = np.zeros((128, NCST), np.float32)
    c[:, C_GATTN:C_GATTN + 16] = g_attn.reshape(16, 128).T
    c[:, C_GFFN:C_GFFN + 16] = g_ffn.reshape(16, 128).T
    c[:, C_GPLE:C_GPLE + 16] = g_ple.reshape(16, 128).T
    c[:, C_GPO:C_GPO + 16] = g_ple_out.reshape(16, 128).T
    c[:, C_QNS] = np.tile(qn_swa, 2)
    c[:, C_KNS] = np.tile(kn_swa, 2)
    c[:, C_QND] = np.tile(qn_diff, 2)
    c[:, C_KND] = np.tile(kn_diff, 2)
    c[:, C_GSUB] = g_sub
    for g in range(2):
        for i in range(4):
            c[0:64, C_SINK + g * 4 + i] = sinks[8 * g + 2 * i]
            c[64:128, C_SINK + g * 4 + i] = sinks[8 * g + 2 * i + 1]
    c[:, C_LQ1:C_LQ1 + 64] = lq1[None, :]
    c[:, C_LK1:C_LK1 + 64] = lk1[None, :]
    c[:, C_LQ2:C_LQ2 + 64] = lq2[None, :]
    c[:, C_LK2:C_LK2 + 64] = lk2[None, :]
    return c


def kernel(x, p, g_attn, w_in, qn_swa, kn_swa, sinks, qn_diff, kn_diff,
           lambda_q1, lambda_k1, lambda_q2, lambda_k2, g_sub, w_out,
           g_ffn, w_gate, w_up, w_down, g_ple, w_ple_gate, w_ple_proj, g_ple_out):
    f = lambda a: np.ascontiguousarray(np.asarray(a, dtype=np.float32))
    x = f(x)
    p = f(p)
    cst = _pack_consts(f(g_attn)[0], f(g_ffn)[0], f(g_ple)[0], f(g_ple_out)[0], f(qn_swa)[0], f(kn_swa)[0],
                       f(qn_diff)[0], f(kn_diff)[0], f(g_sub)[0], f(sinks)[0], f(lambda_q1)[0], f(lambda_k1)[0],
                       f(lambda_q2)[0], f(lambda_k2)[0])
    tb_swa, aug, tb_mask = _tables()
    shared = {
        "w_in": f(w_in)[0], "w_out": f(w_out)[0], "w_gate": f(w_gate)[0], "w_up": f(w_up)[0],
        "w_down": f(w_down)[0], "w_pg": f(w_ple_gate)[0], "w_pp": f(w_ple_proj)[0],
        "cst": cst, "tb_swa": tb_swa, "aug": aug, "tb_mask": tb_mask,
    }
    in_maps = []
    for b in range(8):
        m = dict(shared)
        m["xT"] = np.ascontiguousarray(x[b].T)
        m["pT"] = np.ascontiguousarray(p[0, b].T)
        in_maps.append(m)
    nc = build_nc()
    res = run_bass_kernel_spmd(nc, in_maps, core_ids=list(range(8)))
    out = np.stack([np.ascontiguousarray(np.asarray(r["outT"]).T) for r in res.results], axis=0)
    return out.astype(np.float32)
```

```python
import numpy as np
from contextlib import ExitStack
import concourse.bass as bass
import concourse.mybir as mybir
from concourse.bass_utils import run_bass_kernel_spmd

F32 = mybir.dt.float32
BF16 = mybir.dt.bfloat16
ALU = mybir.AluOpType
AF = mybir.ActivationFunctionType

S = 2048
D = 2048
DFF = 5632
NF = DFF // 128
EPS = 1e-6
NEG = -30000.0
COMPUTE = ('pe', 'act', 'dve', 'pool', 'sp')


class Prog:
    def __init__(self, nc):
        self.nc = nc
        self.engs = {e: [] for e in COMPUTE}
        self.lanes = {e: [] for e in COMPUTE}
        self.res_w = {}
        self.res_r = {}
        self.bar = set()

    def barrier(self):
        self.bar = {(ln, len(lst) - 1) for ln, lst in self.lanes.items() if lst}

    def op(self, eng, fn, reads=(), writes=(), lane=None):
        deps = set()
        for r in reads:
            w = self.res_w.get(r)
            if w is not None:
                deps.add(w)
        for w in writes:
            lw = self.res_w.get(w)
            if lw is not None:
                deps.add(lw)
            for rd in self.res_r.get(w, ()):
                deps.add(rd)
        ln = lane or eng
        if ln not in self.lanes:
            self.lanes[ln] = []
        nd = set()
        for (dl, di) in deps:
            if dl not in COMPUTE:
                if dl == ln:
                    continue
                di = len(self.lanes[dl]) - 1
            nd.add((dl, di))
        deps = nd | self.bar
        idx = len(self.lanes[ln])
        ent = dict(eng=eng, fn=fn, deps=deps, lane=ln, idx=idx, flag=(lane is not None),
                   dma=(lane is not None))
        self.lanes[ln].append(ent)
        self.engs[eng].append(ent)
        me = (ln, idx)
        for r in reads:
            self.res_r.setdefault(r, []).append(me)
        for w in writes:
            self.res_w[w] = me
            self.res_r[w] = []
        return me

    def emit(self):
        nc = self.nc
        for e in self.engs['pe']:
            e['deps'] = {d for d in e['deps'] if d[0] != 'pe'}
        for eng in COMPUTE:
            for e in self.engs[eng]:
                for (ln, i) in e['deps']:
                    self.lanes[ln][i]['flag'] = True
        for ln, lst in self.lanes.items():
            if lst:
                lst[-1]['flag'] = True
        for ln, lst in self.lanes.items():
            c = 0
            for e in lst:
                if e['flag']:
                    c += 16 if e['dma'] else 1
                e['cnt'] = c
        with ExitStack() as st:
            sems = {}
            for ln, lst in self.lanes.items():
                if lst:
                    sems[ln] = st.enter_context(nc.semaphore("s_" + ln))
            block = st.enter_context(nc.Block())
            totals = {ln: lst[-1]['cnt'] for ln, lst in self.lanes.items() if lst}

            def run(engname, engobj):
                waited = {}
                for e in self.engs[engname]:
                    for (ln, i) in sorted(e['deps']):
                        v = self.lanes[ln][i]['cnt']
                        if waited.get(ln, 0) < v:
                            engobj.wait_ge(sems[ln], v)
                            waited[ln] = v
                    ins = e['fn'](engobj)
                    if e['flag']:
                        ins.then_inc(sems[e['lane']], 16 if e['dma'] else 1)
                if engname == 'sp':
                    for ln, v in totals.items():
                        if waited.get(ln, 0) < v:
                            engobj.wait_ge(sems[ln], v)

            if self.engs['pe']:
                block.tensor(lambda eng: run('pe', eng))
            if self.engs['act']:
                block.scalar(lambda eng: run('act', eng))
            if self.engs['dve']:
                block.vector(lambda eng: run('dve', eng))
            if self.engs['pool']:
                block.gpsimd(lambda eng: run('pool', eng))
            block.sync(lambda eng: run('sp', eng))


C_GATTN, C_GFFN, C_GPLE, C_GPO = 0, 16, 32, 48
C_QNS, C_KNS, C_QND, C_KND, C_GSUB = 64, 65, 66, 67, 68
C_SINK = 69
C_LQ1, C_LK1, C_LQ2, C_LK2 = 77, 141, 205, 269
NCST = 336


def build_nc():
    nc = bass.Bass("TRN2", target_bir_lowering=False)
    dt_in = lambda n, s, d=F32: nc.dram_tensor(n, s, d, kind="ExternalInput").ap()
    xT = dt_in("xT", [D, S])
    pT = dt_in("pT", [256, S])
    w_in = dt_in("w_in", [D, 4352])
    w_out = dt_in("w_out", [D, D])
    w_gate = dt_in("w_gate", [D, DFF])
    w_up = dt_in("w_up", [D, DFF])
    w_down = dt_in("w_down", [DFF, D])
    w_pg = dt_in("w_pg", [D, D])
    w_pp = dt_in("w_pp", [256, D])
    cstd = dt_in("cst", [128, NCST])
    tb_swa = dt_in("tb_swa", [128, 4096])
    aug = dt_in("aug", [48, S])
    tb_mask = dt_in("tb_mask", [128, 128])
    outT = nc.dram_tensor("outT", [D, S], F32, kind="ExternalOutput").ap()
    hbuf = nc.dram_tensor("hbuf", [D, S], F32, kind="Internal").ap()
    ybuf = nc.dram_tensor("ybuf", [D, S], BF16, kind="Internal").ap()
    ubuf = nc.dram_tensor("ubuf", [D, S], BF16, kind="Internal").ap()

    ARENA_B = 194 * 1024

    with ExitStack() as st:
        T = lambda name, shape, dt: st.enter_context(nc.sbuf_tensor(name, shape, dt))
        arena = T("arena", [128, ARENA_B // 2], BF16)
        cst = T("cstt", [128, NCST], F32)
        drv = T("drv", [128, 32], F32)
        ones_f = T("ones_f", [128, 128], F32)
        blk_f = T("blk_f", [128, 128], F32)
        ones_b = T("ones_b", [128, 128], BF16)
        rstd_all = T("rstd_all", [128, S], F32)
        ps = [st.enter_context(nc.psum_tensor("ps%d" % i, [128, 512], F32)) for i in range(8)]
        p = Prog(nc)
        PSR = ['ps%d' % i for i in range(8)]

        class Arena:
            def __init__(self):
                self.off = 0

            def reset(self):
                self.off = 0

            def alloc(self, shape, dt):
                esz = 2 if dt == BF16 else 4
                n = int(np.prod(shape[1:]))
                nb = n * esz
                assert nb % 4 == 0
                v = arena[:, self.off // 2:(self.off + nb) // 2]
                if dt != BF16:
                    v = v.bitcast(dt)
                if len(shape) == 3:
                    v = v.rearrange("p (a b) -> p a b", a=shape[1], b=shape[2])
                elif len(shape) == 4:
                    v = v.rearrange("p (a b c) -> p a b c", a=shape[1], b=shape[2], c=shape[3])
                self.off += nb
                assert self.off <= ARENA_B, ("arena overflow", self.off)
                return v

        A = Arena()

        def MM(out, lhsT, rhs, start, stop, reads, writes):
            p.op('pe', lambda e: e.matmul(out, lhsT=lhsT, rhs=rhs, start=start, stop=stop), reads, writes)

        def ACT(out, in_, func, reads, writes, **kw):
            p.op('act', lambda e: e.activation(out=out, in_=in_, func=func, **kw), reads, writes)

        def TT(out, in0, in1, op, reads, writes, eng='dve'):
            p.op(eng, lambda e: e.tensor_tensor(out=out, in0=in0, in1=in1, op=op), reads, writes)

        def STT(out, in0, scalar, in1, op0, op1, reads, writes):
            p.op('dve', lambda e: e.scalar_tensor_tensor(out=out, in0=in0, scalar=scalar, in1=in1, op0=op0, op1=op1),
                 reads, writes)

        def TS(out, in0, s1, s2, op0, op1, reads, writes):
            p.op('dve', lambda e: e.tensor_scalar(out=out, in0=in0, scalar1=s1, scalar2=s2, op0=op0, op1=op1),
                 reads, writes)

        def DMA(eng, out, in_, reads, writes, lane):
            p.op(eng, lambda e: e.dma_start(out=out, in_=in_), reads, writes, lane=lane)

        class Ring:
            def __init__(self, name, bufs):
                self.name, self.bufs, self.i = name, bufs, 0

            def next(self):
                k = self.i % len(self.bufs)
                self.i += 1
                return self.bufs[k], "%s%d" % (self.name, k), "d_%s%d" % (self.name, k)

        DMA('sp', cst[:], cstd, [], ['cst'], 'd_cst')
        p.op('dve', lambda e: e.memset(ones_f[:], 1.0), [], ['ones_f'])
        p.op('dve', lambda e: e.memset(blk_f[:], 0.0), [], ['blk_f'])
        p.op('dve', lambda e: e.memset(blk_f[0:64, 0:64], 1.0), [], ['blk_f'])
        p.op('dve', lambda e: e.memset(blk_f[64:128, 64:128], 1.0), [], ['blk_f'])
        p.op('dve', lambda e: e.memset(ones_b[:], 1.0), [], ['ones_b'])
        ones_h = [T("ones_h%d" % i, [128, 128], BF16) for i in range(2)]
        for i in range(2):
            p.op('dve', lambda e, i=i: e.memset(ones_h[i][:], 0.0), [], ['ones_b'])
            p.op('dve', lambda e, i=i: e.memset(ones_h[i][:, i * 64:(i + 1) * 64], 1.0), [], ['ones_b'])
        TS(drv[:, 0:1], cst[:, C_QNS:C_QNS + 1], 0.125, None, ALU.mult, ALU.bypass, ['cst'], ['drv'])
        TS(drv[:, 1:2], cst[:, C_QND:C_QND + 1], 0.125, None, ALU.mult, ALU.bypass, ['cst'], ['drv'])
        TS(drv[:, 2:3], cst[:, C_GSUB:C_GSUB + 1], 0.8, None, ALU.mult, ALU.bypass, ['cst'], ['drv'])
        lam_t = T("lam_t", [128, 128], F32)
        TT(lam_t[:, 0:64], cst[:, C_LQ1:C_LQ1 + 64], cst[:, C_LK1:C_LK1 + 64], ALU.mult, ['cst'], ['lam_t'])
        TT(lam_t[:, 64:128], cst[:, C_LQ2:C_LQ2 + 64], cst[:, C_LK2:C_LK2 + 64], ALU.mult, ['cst'], ['lam_t'])
        p.op('dve', lambda e: e.reduce_sum(out=drv[:, 12:13], in_=lam_t[:, 0:64], axis=mybir.AxisListType.X),
             ['lam_t'], ['drv'])
        p.op('dve', lambda e: e.reduce_sum(out=drv[:, 13:14], in_=lam_t[:, 64:128], axis=mybir.AxisListType.X),
             ['lam_t', 'drv'], ['drv'])
        ACT(drv[:, 14:16], drv[:, 12:14], AF.Exp, ['drv'], ['drv'])
        ACT(drv[:, 4:12], cst[:, C_SINK:C_SINK + 8], AF.Exp, ['cst', 'drv'], ['drv'])
        TT(drv[:, 3:4], drv[:, 15:16], drv[:, 14:15], ALU.subtract, ['drv'], ['drv'])
        TS(drv[:, 3:4], drv[:, 3:4], -0.2, None, ALU.add, ALU.bypass, ['drv'], ['drv'])

        def rstd_from(ps_ap, out_ap, tmp_ap, n, reads, tmp_res, out_res):
            ACT(tmp_ap, ps_ap, AF.Ln, reads, [tmp_res], scale=1.0 / n, bias=EPS)
            ACT(out_ap, tmp_ap, AF.Exp, [tmp_res], [out_res], scale=-0.5)

        A.reset()
        uT = A.alloc([128, 16, S], BF16)
        markB = A.off
        xin = Ring('xin', [A.alloc([128, 16, 512], F32) for _ in range(2)])
        sqr = Ring('sqa', [A.alloc([128, 512], F32) for _ in range(4)])
        lnb = Ring('lna', [A.alloc([128, 512], F32) for _ in range(2)])
        rsb = Ring('rsa', [A.alloc([128, 512], F32) for _ in range(2)])
        xv = xT.rearrange("(kc p) s -> p kc s", p=128)
        for tc in range(4):
            xt, xr, xl = xin.next()
            DMA('sp', xt, xv[:, :, tc * 512:(tc + 1) * 512], [], [xr], xl)
            b = tc % 2
            for k in range(16):
                sq, sr, _ = sqr.next()
                ACT(sq, xt[:, k, :], AF.Square, [xr], [sr])
                MM(ps[b][:], ones_f[:], sq, k == 0, k == 15, ['ones_f', sr], [PSR[b]])
            lt, lr, _ = lnb.next()
            rt, rr, _ = rsb.next()
            rstd_from(ps[b][:], rt, lt, D, [PSR[b]], lr, rr)
            for k in range(16):
                STT(uT[:, k, tc * 512:(tc + 1) * 512], xt[:, k, :], cst[:, C_GATTN + k:C_GATTN + k + 1], rt,
                    ALU.mult, ALU.mult, [xr, rr, 'cst'], [('uT', k, tc)])
        UT_ALL = [('uT', k, tc) for k in range(16) for tc in range(4)]

        p.barrier()
        A.off = markB
        tbl = A.alloc([128, 5120], F32)
        unit_q = A.alloc([128, 4, S], BF16)
        kp = [A.alloc([128, S], BF16) for _ in range(2)]
        unit_v = A.alloc([128, 16, 128], BF16)
        unit_v2 = A.alloc([128, 16, 128], BF16)
        p.op('pool', lambda e: e.memset(kp[0][64:128, :], 0.0), [], ['unit_k'])
        p.op('pool', lambda e: e.memset(kp[1][0:64, :], 0.0), [], ['unit_k'])
        p.op('pool', lambda e: e.memset(unit_v[:, :, 64:128], 0.0), [], ['unit_v'])
        p.op('pool', lambda e: e.memset(unit_v2[:, :, 0:64], 0.0), [], ['unit_v'])
        wrB = Ring('wrB', [A.alloc([128, 16, 256], BF16) for _ in range(2)])
        wk32 = Ring('wk32_', [A.alloc([128, 512], F32) for _ in range(10)])
        wk16 = Ring('wk16_', [A.alloc([128, 512], BF16) for _ in range(4)])
        ystage = Ring('yst', [A.alloc([128, 4, S], BF16)])
        fin32 = Ring('fin32_', [A.alloc([128, 512], F32) for _ in range(10)])
        sctr = [0]

        def run_pipeline(steps, front, back, L):
            q = []
            deferred = []

            def tick():
                for d in deferred:
                    d[0] -= 1
                ready = [d for d in deferred if d[0] <= 0]
                deferred[:] = [d for d in deferred if d[0] > 0]
                for d in ready:
                    d[1]()

            for stp in steps:
                q.append((stp, front(stp)))
                if len(q) > L:
                    s0, i0 = q.pop(0)
                    for (dl, fn) in back(s0, i0):
                        deferred.append([dl, fn])
                tick()
            while q:
                s0, i0 = q.pop(0)
                for (dl, fn) in back(s0, i0):
                    deferred.append([dl, fn])
                tick()
            while deferred:
                tick()
        w_in_v = w_in.rearrange("(kc p) n -> p kc n", p=128)

        zsets = [(0, 1), (2, 3), (4, 5)]
        zctr = [0]
        ssb = [0]
        pending = []

        def flush_one():
            (zb, dsts, dres, gcol, blk) = pending.pop(0)
            for j in range(2):
                sq, sr = zb[2 + j]
                sb = 6 + (ssb[0] % 2)
                ssb[0] += 1
                MM(ps[sb][:], blk[:], sq, True, True, ['blk_f', 'ones_f', sr], [PSR[sb]])
                lt, lr, _ = wk32.next()
                rt, rr, _ = wk32.next()
                rstd_from(ps[sb][:], rt, lt, 64 if blk is blk_f else 128, [PSR[sb]], lr, rr)
                for (r0, r1, dap) in dsts[j]:
                    STT(dap, ps[zb[j]][r0:r1, :], gcol[r0:r1, :], rt[r0:r1, :], ALU.mult, ALU.mult,
                        [PSR[zb[j]], rr, 'drv', 'cst'], [dres])

        def proj_fm_norm(wt, wres, c0, gcol, dst_fn, dres):
            for hc in range(2):
                zb = zsets[zctr[0] % 3]
                zctr[0] += 1
                for k in range(16):
                    for j in range(2):
                        tcx = hc * 2 + j
                        MM(ps[zb[j]][:], wt[:, k, c0:c0 + 128], uT[:, k, tcx * 512:(tcx + 1) * 512],
                           k == 0, k == 15, [wres, ('uT', k, tcx)], [PSR[zb[j]]])
                sqs = []
                for j in range(2):
                    sq, sr, _ = wk32.next()
                    ACT(sq, ps[zb[j]][:], AF.Square, [PSR[zb[j]]], [sr])
                    sqs.append((sq, sr))
                if pending:
                    flush_one()
                pending.append(((zb[0], zb[1], sqs[0], sqs[1]),
                                [dst_fn(hc * 2), dst_fn(hc * 2 + 1)], dres, gcol, blk_f))

        def flush_all():
            while pending:
                flush_one()

        def proj_tm(wt, wres, c0, ncol, dst, dres):
            per = 512 // ncol
            nb = 16 // per
            for tb in range(16):
                bk = tb // per
                co = (tb % per) * ncol
                for k in range(16):
                    MM(ps[bk][:, co:co + ncol], uT[:, k, tb * 128:(tb + 1) * 128], wt[:, k, c0:c0 + ncol],
                       k == 0, k == 15, [wres, ('uT', k, tb // 4)], [PSR[bk]])
            ei = 0
            for bk in range(nb):
                src = ps[bk][:].rearrange("p (a b) -> p a b", a=per, b=ncol)
                for (dt_, dc, sc, wd) in dst:
                    dv = dt_[:, bk * per:(bk + 1) * per, dc:dc + wd]
                    sv = src[:, :, sc:sc + wd]
                    if ei % 2 == 0:
                        p.op('dve', lambda e, dv=dv, sv=sv: e.tensor_copy(out=dv, in_=sv), [PSR[bk]], [dres])
                    else:
                        ACT(dv, sv, AF.Copy, [PSR[bk]], [dres])
                    ei += 1

        def load_w(ring, dst_cols, src_ap_list):
            wt, wres, wl = ring.next()
            for (c0, c1), src in zip(dst_cols, src_ap_list):
                DMA('pool', wt[:, :, c0:c1], src, [], [wres], wl)
            return wt, wres

        DMA('sp', tbl[:, 0:4096], tb_swa, [], ['tbl'], 'd_tbl')
        tblv = tbl[:, 0:4096].rearrange("p (a b) -> p a b", a=8, b=512)
        for g in range(2):
            yt, yr, yl = ystage.next()
            for half in range(2):
                c = g * 512 + half * 256
                wt, wres = load_w(wrB, [(0, 256)], [w_in_v[:, :, c:c + 256]])
                for cc in range(2):
                    qi = half * 2 + cc
                    proj_fm_norm(wt, wres, cc * 128, drv[:, 0:1],
                                 lambda tc, qi=qi: [(0, 128, unit_q[:, qi, tc * 512:(tc + 1) * 512])], 'unit_q')
            kc0 = 1024 + g * 64
            vc0 = 1152 + g * 64
            wt, wres = load_w(wrB, [(0, 64), (64, 128), (128, 192)],
                              [w_in_v[:, :, kc0:kc0 + 64], w_in_v[:, :, kc0:kc0 + 64], w_in_v[:, :, vc0:vc0 + 64]])
            proj_fm_norm(wt, wres, 0, cst[:, C_KNS:C_KNS + 1],
                         lambda tc: [(0, 64, kp[0][0:64, tc * 512:(tc + 1) * 512]), (64, 128, kp[1][64:128, tc * 512:(tc + 1) * 512])], 'unit_k')
            flush_all()
            proj_tm(wt, wres, 128, 64, [(unit_v, 0, 0, 64), (unit_v2, 64, 0, 64)], 'unit_v')
            esb = drv[:, 4 + g * 4:8 + g * 4]
            es_bc = bass.AP(esb.tensor, esb.offset, [list(esb.ap[0]), [1, 4], [0, 128]])
            steps = []
            for n in range(16):
                kbs = [n - 1, n] if n > 0 else [n]
                for par in range(2):
                    for kb in kbs:
                        steps.append((n, par, kb, par == 0 and kb == kbs[0], par == 1 and kb == kbs[-1],
                                      par == 1 and kb == kbs[-1]))

            def swa_front(stp, g=g):
                (n, par, kb, first, last, fin) = stp
                kind = 0 if kb == n else 1
                sb = sctr[0] % 4
                sctr[0] += 1
                MM(ps[sb][:], kp[par][:, kb * 128:(kb + 1) * 128],
                   unit_q[:, :, n * 128:(n + 1) * 128], True, True,
                   ['unit_k', 'unit_q'], [PSR[sb]])
                tt, tr, _ = wk32.next()
                TT(tt, ps[sb][:], tblv[:, kind * 4 + g * 2 + par, :], ALU.add, [PSR[sb], 'tbl'], [tr])
                et, er, _ = wk16.next()
                ACT(et, tt, AF.Exp, [tr], [er])
                return (et, er)

            def swa_back(stp, info, yt=yt, yr=yr):
                (n, par, kb, first, last, fin) = stp
                (et, er) = info
                pvb = 4 + (n % 2)
                dnb = 6 + (n % 2)
                MM(ps[pvb][:], (unit_v if par == 0 else unit_v2)[:, kb, :], et, first, last,
                   ['unit_v', er], [PSR[pvb]])
                MM(ps[dnb][:], ones_h[par][:], et, first, last,
                   ['ones_b', er], [PSR[dnb]])
                if not fin:
                    return []
                dd, dr, _ = fin32.next()
                lt, lr, _ = fin32.next()
                rt, rr, _ = fin32.next()

                def stage_a():
                    TT(dd.rearrange("p (a b) -> p a b", a=4, b=128),
                       ps[dnb][:].rearrange("p (a b) -> p a b", a=4, b=128),
                       es_bc, ALU.add, [PSR[dnb], 'drv'], [dr])
                    ACT(lt, dd, AF.Ln, [dr], [lr])
                    ACT(rt, lt, AF.Exp, [lr], [rr], scale=-1.0)

                def stage_b():
                    TT(yt[:, :, n * 128:(n + 1) * 128], ps[pvb][:].rearrange("p (a b) -> p a b", a=4, b=128),
                       rt.rearrange("p (a b) -> p a b", a=4, b=128), ALU.mult, [PSR[pvb], rr], [yr])
                return [(1, stage_a), (3, stage_b)]

            run_pipeline(steps, swa_front, swa_back, 3)
            for i in range(4):
                ch = g * 4 + i
                DMA('sp', ybuf[ch * 128:(ch + 1) * 128, :], yt[:, i, :], [yr], [('ybuf', ch)], yl)

        maskt = tbl[:, 0:128]
        DMA('sp', maskt, tb_mask, [], ['tbl'], 'd_tbl')
        maskb = tbl[:, 128:192].bitcast(BF16)
        TS(maskb, maskt, -1.0, None, ALU.is_ge, ALU.bypass, ['tbl'], ['maskb'])
        qp = [unit_q[:, 0, :], unit_q[:, 1, :]]
        p.op('pool', lambda e: e.memset(unit_q[:, 0:2, :], 0.0), [], ['unit_q'])
        aug_v = aug.rearrange("(h a r) s -> h a r s", h=8, a=2, r=3)
        for h in range(8):
            slope = 2.0 ** (-(h + 1))
            yt, yr, yl = ystage.next()
            qc0 = 1280 + h * 128
            kc0 = 2304 + h * 128
            vc0 = 3328 + h * 128
            DMA('pool', kp[0][64:67, :], aug_v[h, 0], [], ['aug'], 'd_aug')
            DMA('pool', kp[1][0:3, :], aug_v[h, 0], [], ['aug'], 'd_aug')
            DMA('pool', qp[0][64:67, :], aug_v[h, 1], [], ['aug'], 'd_aug')
            DMA('pool', qp[1][0:3, :], aug_v[h, 1], [], ['aug'], 'd_aug')
            wt, wres = load_w(wrB, [(0, 128), (128, 256)], [w_in_v[:, :, qc0:qc0 + 128], w_in_v[:, :, kc0:kc0 + 128]])
            proj_fm_norm(wt, wres, 0, drv[:, 1:2],
                         lambda tc: [(0, 64, qp[0][0:64, tc * 512:(tc + 1) * 512]),
                                     (64, 128, qp[1][64:128, tc * 512:(tc + 1) * 512])], 'unit_q')
            proj_fm_norm(wt, wres, 128, cst[:, C_KND:C_KND + 1],
                         lambda tc: [(0, 64, kp[0][0:64, tc * 512:(tc + 1) * 512]),
                                     (64, 128, kp[1][64:128, tc * 512:(tc + 1) * 512])], 'unit_k')
            flush_all()
            if h % 2 == 0:
                wt, wres = load_w(wrB, [(0, 256)], [w_in_v[:, :, vc0:vc0 + 256]])
                proj_tm(wt, wres, 0, 256, [(unit_v, 0, 0, 128), (unit_v2, 0, 128, 128)], 'unit_v')
            vd = unit_v if h % 2 == 0 else unit_v2
            steps = []
            for c in range(4):
                for m in range(2):
                    pi = c * 2 + m
                    for j in range(4 * c + 4):
                        steps.append((c, m, j, 4 + 2 * (pi % 2), 5 + 2 * (pi % 2)))
            ats = {}

            def diff_front(stp, h=h, slope=slope):
                (c, m, j, pvb, dnb) = stp
                diag = j >= 4 * c
                jk = j - 4 * c if diag else 0
                N = 512 - jk * 128
                off = 512 - N
                q0 = c * 512 + off
                sb = sctr[0] % 4
                sctr[0] += 1
                MM(ps[sb][:, 0:N], kp[m][:, j * 128:(j + 1) * 128],
                   qp[m][:, q0:q0 + N], True, True, ['unit_k', 'unit_q', 'aug'], [PSR[sb]])
                cb = float(-slope * 128.0 * (4 * c - j))
                et, er, _ = wk16.next()
                ACT(et[:, 0:N], ps[sb][:, 0:N], AF.Exp, [PSR[sb]], [er], bias=cb)
                if diag:
                    TT(et[:, 0:128], et[:, 0:128], maskb, ALU.mult, ['maskb'], [er])
                return (et, er, N, off)

            def diff_back(stp, info, yt=yt, yr=yr, vd=vd):
                (c, m, j, pvb, dnb) = stp
                (et, er, N, off) = info
                last_j = 4 * c + 3
                MM(ps[pvb][:, off:512], vd[:, j, :], et[:, 0:N], j == 0, j == last_j, ['unit_v', er], [PSR[pvb]])
                MM(ps[dnb][:, off:512], ones_b[:], et[:, 0:N], j == 0, j == last_j, ['ones_b', er], [PSR[dnb]])
                if j != last_j:
                    return []
                lt, lr, _ = fin32.next()
                rt, rr, _ = fin32.next()
                at, ar, _ = fin32.next()
                ats[(c, m)] = (at, ar)

                def stage_a():
                    ACT(lt, ps[dnb][:], AF.Ln, [PSR[dnb]], [lr])
                    ACT(rt, lt, AF.Exp, [lr], [rr], scale=-1.0)

                def stage_b():
                    TT(at, ps[pvb][:], rt, ALU.mult, [PSR[pvb], rr], [ar])
                outl = [(1, stage_a), (3, stage_b)]
                if m == 1:
                    od, odr, _ = fin32.next()
                    sq, sr, _ = fin32.next()
                    lt2, lr2, _ = fin32.next()
                    rt2, rr2, _ = fin32.next()

                    def stage_c():
                        a0, a0r = ats[(c, 0)]
                        a1, a1r = ats[(c, 1)]
                        STT(od, a1, drv[:, 3:4], a0, ALU.mult, ALU.add, [a0r, a1r, 'drv'], [odr])
                        ACT(sq, od, AF.Square, [odr], [sr])

                    def stage_d():
                        sb = sctr[0] % 4
                        sctr[0] += 1
                        MM(ps[sb][:], ones_f[:], sq, True, True, ['ones_f', sr], [PSR[sb]])
                        rstd_from(ps[sb][:], rt2, lt2, 128, [PSR[sb]], lr2, rr2)

                    def stage_e():
                        STT(yt[:, 0, c * 512:(c + 1) * 512], od, drv[:, 2:3], rt2, ALU.mult, ALU.mult,
                            [odr, rr2, 'drv'], [yr])
                    outl += [(4, stage_c), (5, stage_d), (7, stage_e)]
                return outl

            run_pipeline(steps, diff_front, diff_back, 3)
            ch = 8 + h
            DMA('sp', ybuf[ch * 128:(ch + 1) * 128, :], yt[:, 0, :], [yr], [('ybuf', ch)], yl)

        p.barrier()
        A.reset()
        yT = A.alloc([128, 16, S], BF16)
        xc = Ring('xc', [A.alloc([128, S], F32) for _ in range(2)])
        hc_r = Ring('hc', [A.alloc([128, S], F32) for _ in range(2)])
        sq_r = Ring('sqc', [A.alloc([128, S], F32) for _ in range(2)])
        accsq = A.alloc([128, S], F32)
        wrC = Ring('wrC', [A.alloc([128, 16, 512], BF16) for _ in range(2)])
        tmpc = A.alloc([128, S], F32)
        ust_r = Ring('ust', [A.alloc([128, S], BF16) for _ in range(2)])
        yv = ybuf.rearrange("(kc p) s -> p kc s", p=128)
        for gk in range(4):
            DMA('sp', yT[:, gk * 4:(gk + 1) * 4, :], yv[:, gk * 4:(gk + 1) * 4, :],
                [('ybuf', k) for k in range(gk * 4, gk * 4 + 4)], [('yT', gk)], 'd_yT%d' % gk)
        w_out_v = w_out.rearrange("(kc p) n -> p kc n", p=128)
        def ld_x(n):
            xt, xr, xl = xc.next()
            DMA('sp', xt, xT[n * 128:(n + 1) * 128, :], [], [xr], xl)
            return xt, xr
        xq = [ld_x(0)]
        wq = [load_w(wrC, [(0, 512)], [w_out_v[:, :, 0:512]])]
        for sn in range(4):
            if sn + 1 < 4:
                wq.append(load_w(wrC, [(0, 512)], [w_out_v[:, :, (sn + 1) * 512:(sn + 2) * 512]]))
            wt, wres = wq.pop(0)
            for n4 in range(4):
                n = sn * 4 + n4
                banks = (0, 1, 2, 3) if n % 2 == 0 else (4, 5, 6, 7)
                if n + 1 < 16:
                    xq.append(ld_x(n + 1))
                for k in range(16):
                    for tc in range(4):
                        MM(ps[banks[tc]][:], wt[:, k, n4 * 128:(n4 + 1) * 128], yT[:, k, tc * 512:(tc + 1) * 512],
                           k == 0, k == 15, [wres, ('yT', k // 4)], [PSR[banks[tc]]])
                xt, xr = xq.pop(0)
                ht, hr, hl = hc_r.next()
                for tc in range(4):
                    TT(ht[:, tc * 512:(tc + 1) * 512], ps[banks[tc]][:], xt[:, tc * 512:(tc + 1) * 512], ALU.add,
                       [PSR[banks[tc]], xr], [hr])
                DMA('sp', hbuf[n * 128:(n + 1) * 128, :], ht, [hr], [('hbuf', n, 0), ('hbuf', n, 1)], hl)
                ut, utr, utl = ust_r.next()
                ACT(ut, ht, AF.Copy, [hr, 'cst'], [utr], scale=cst[:, C_GFFN + n:C_GFFN + n + 1])
                DMA('sp', ubuf[n * 128:(n + 1) * 128, :], ut, [utr], [('ubuf', n)], utl)
                if n == 0:
                    ACT(accsq, ht, AF.Square, [hr], ['accsq'])
                else:
                    sq, sr, _ = sq_r.next()
                    ACT(sq, ht, AF.Square, [hr], [sr])
                    TT(accsq, accsq, sq, ALU.add, [sr], ['accsq'], eng='pool')

        def finish_rstd(acc_ap, acc_res, ntok, tmp_ap, out_ap, out_res, banks):
            for i in range(ntok // 512):
                b = banks[i]
                MM(ps[b][:], ones_f[:], acc_ap[:, i * 512:(i + 1) * 512], True, True, ['ones_f', acc_res], [PSR[b]])
                rstd_from(ps[b][:], out_ap[:, i * 512:(i + 1) * 512], tmp_ap[:, i * 512:(i + 1) * 512], D,
                          [PSR[b]], 'tmp_rs', out_res)

        finish_rstd(accsq, 'accsq', S, tmpc, rstd_all, 'rstd_all', (0, 1, 2, 3))

        w_gate_v = w_gate.rearrange("(kc p) n -> p kc n", p=128)
        w_up_v = w_up.rearrange("(kc p) n -> p kc n", p=128)
        w_down_v = w_down.rearrange("(f p) n -> p f n", p=128)
        w_pg_v = w_pg.rearrange("(kc p) n -> p kc n", p=128)
        w_pp_v = w_pp.rearrange("(kc p) n -> p kc n", p=128)
        pT_v = pT.rearrange("(kc p) s -> p kc s", p=128)
        HT = 1024

        def make_u(u, hf, gc0, rstd_ap, rstd_res, hin):
            for k in range(16):
                it, ir, il = hin.next()
                DMA('sp', it, hbuf[k * 128:(k + 1) * 128, hf * HT:(hf + 1) * HT], [('hbuf', k, hf)], [ir], il)
                STT(u[:, k, :], it, cst[:, gc0 + k:gc0 + k + 1], rstd_ap, ALU.mult, ALU.mult,
                    [ir, rstd_res, 'cst'], [('u', k)])

        for hf in range(2):
            p.barrier()
            A.reset()
            u = A.alloc([128, 16, HT], BF16)
            aT = A.alloc([128, NF, HT], BF16)
            markW = A.off
            ubv = ubuf.rearrange("(kc p) s -> p kc s", p=128)
            for gk in range(4):
                DMA('sp', u[:, gk * 4:(gk + 1) * 4, :], ubv[:, gk * 4:(gk + 1) * 4, hf * HT:(hf + 1) * HT],
                    [('ubuf', k) for k in range(gk * 4, gk * 4 + 4)], [('u', k) for k in range(gk * 4, gk * 4 + 4)],
                    'd_u%d' % gk)
            rs2 = rstd_all[:, hf * HT:(hf + 1) * HT]
            wrg = Ring('wrg', [A.alloc([128, 16, 256], BF16) for _ in range(2)])
            wru = Ring('wru', [A.alloc([128, 16, 256], BF16) for _ in range(2)])
            sil = Ring('sil', [A.alloc([128, 512], F32) for _ in range(9)])
            for fp in range(NF // 2):
                wg, wgr = load_w(wrg, [(0, 256)], [w_gate_v[:, :, fp * 256:(fp + 1) * 256]])
                wu, wur = load_w(wru, [(0, 256)], [w_up_v[:, :, fp * 256:(fp + 1) * 256]])
                for f2 in range(2):
                    f = fp * 2 + f2
                    gb, ub = ((0, 1), (2, 3)) if f % 2 == 0 else ((4, 5), (6, 7))
                    for k in range(16):
                        for tc in range(2):
                            MM(ps[gb[tc]][:], wg[:, k, f2 * 128:(f2 + 1) * 128], u[:, k, tc * 512:(tc + 1) * 512],
                               k == 0, k == 15, [wgr, ('u', k)], [PSR[gb[tc]]])
                    for k in range(16):
                        for tc in range(2):
                            MM(ps[ub[tc]][:], wu[:, k, f2 * 128:(f2 + 1) * 128], u[:, k, tc * 512:(tc + 1) * 512],
                               k == 0, k == 15, [wur, ('u', k)], [PSR[ub[tc]]])
                    for tc in range(2):
                        rsl = rs2[:, tc * 512:(tc + 1) * 512]
                        t1, t1r, _ = sil.next()
                        TT(t1, ps[gb[tc]][:], rsl, ALU.mult, [PSR[gb[tc]], 'rstd_all'], [t1r])
                        stt, sr, _ = sil.next()
                        ACT(stt, t1, AF.Silu, [t1r], [sr])
                        t2, t2r, _ = sil.next()
                        TT(t2, ps[ub[tc]][:], rsl, ALU.mult, [PSR[ub[tc]], 'rstd_all'], [t2r])
                        TT(aT[:, f, tc * 512:(tc + 1) * 512], stt, t2, ALU.mult, [sr, t2r], [('aT', f)])
            p.barrier()
            A.off = markW
            rstd2 = A.alloc([128, HT], F32)
            h1c = Ring('h1c', [A.alloc([128, HT], F32) for _ in range(2)])
            h2c = Ring('h2c', [A.alloc([128, HT], F32) for _ in range(2)])
            sqd = Ring('sqd', [A.alloc([128, HT], F32) for _ in range(1)])
            acc2 = A.alloc([128, HT], F32)
            wrd = Ring('wrd', [A.alloc([128, NF, 256], BF16) for _ in range(2)])
            tmp2 = sqd.bufs[0]
            def ld_wd(ng):
                wt_, wres_, wl_ = wrd.next()
                DMA('pool', wt_, w_down_v[:, :, ng * 256:(ng + 1) * 256], [], [wres_], wl_)
                return wt_, wres_

            def ld_h1(n):
                it_, ir_, il_ = h1c.next()
                DMA('sp', it_, hbuf[n * 128:(n + 1) * 128, hf * HT:(hf + 1) * HT], [('hbuf', n, hf)], [ir_], il_)
                return it_, ir_
            hq = [ld_h1(0)]
            wq = [ld_wd(0)]
            for ng in range(8):
                if ng + 1 < 8:
                    wq.append(ld_wd(ng + 1))
                wd, wdr0 = wq.pop(0)
                for n2 in range(2):
                    n = ng * 2 + n2
                    wdr = wdr0
                    if n + 1 < 16:
                        hq.append(ld_h1(n + 1))
                    bb = ((0, 1), (2, 3), (4, 5), (6, 7))[n % 4]
                    for f in range(NF):
                        for tc in range(2):
                            MM(ps[bb[tc]][:], wd[:, f, n2 * 128:(n2 + 1) * 128], aT[:, f, tc * 512:(tc + 1) * 512],
                               f == 0, f == NF - 1, [wdr, ('aT', f)], [PSR[bb[tc]]])
                    it, ir = hq.pop(0)
                    ot, orr, ol = h2c.next()
                    for tc in range(2):
                        TT(ot[:, tc * 512:(tc + 1) * 512], ps[bb[tc]][:], it[:, tc * 512:(tc + 1) * 512], ALU.add,
                           [PSR[bb[tc]], ir], [orr])
                    DMA('sp', hbuf[n * 128:(n + 1) * 128, hf * HT:(hf + 1) * HT], ot, [orr], [('hbuf', n, hf)], ol)
                    ACT(u[:, n, :], ot, AF.Copy, [orr, 'cst'], [('u', n)], scale=cst[:, C_GPLE + n:C_GPLE + n + 1])
                    if n == 0:
                        ACT(acc2, ot, AF.Square, [orr], ['acc2'])
                    else:
                        sq, sr, _ = sqd.next()
                        ACT(sq, ot, AF.Square, [orr], [sr])
                        TT(acc2, acc2, sq, ALU.add, [sr], ['acc2'], eng='pool')
            finish_rstd(acc2, 'acc2', HT, tmp2, rstd2, 'rstd2', (0, 1))
            p.barrier()
            A.off = 32 * 1024
            pTb = A.alloc([128, 2, HT], BF16)
            wpp = A.alloc([128, 2, D], BF16)
            wrp = Ring('wrp', [A.alloc([128, 16, 512], BF16) for _ in range(2)])
            acc3 = A.alloc([128, HT], F32)
            rstd3 = A.alloc([128, HT], F32)
            tmp3 = A.alloc([128, HT], F32)
            sq3 = Ring('sq3_', [A.alloc([128, 512], F32) for _ in range(2)])
            sg_r = Ring('sg', [A.alloc([128, 512], F32) for _ in range(2)])
            t_r = Ring('tpl', [A.alloc([128, 512], F32) for _ in range(2)])
            h3c = Ring('h3c', [A.alloc([128, HT], F32) for _ in range(2)])
            o3c = Ring('o3c', [A.alloc([128, HT], F32) for _ in range(2)])
            assert A.off <= markW
            DMA('pool', pTb, pT_v[:, :, hf * HT:(hf + 1) * HT], [], ['pTb'], 'd_pTb')
            DMA('pool', wpp, w_pp_v, [], ['wpp'], 'd_wpp')
            for n in range(16):
                for tc in range(2):
                    b = (n * 2 + tc) % 4
                    for kc in range(2):
                        MM(ps[b][:], wpp[:, kc, n * 128:(n + 1) * 128], pTb[:, kc, tc * 512:(tc + 1) * 512],
                           kc == 0, kc == 1, ['wpp', 'pTb'], [PSR[b]])
                    if n == 0:
                        ACT(acc3[:, tc * 512:(tc + 1) * 512], ps[b][:], AF.Square, [PSR[b]], [('acc3', tc)])
                    else:
                        sq, sr, _ = sq3.next()
                        ACT(sq, ps[b][:], AF.Square, [PSR[b]], [sr])
                        TT(acc3[:, tc * 512:(tc + 1) * 512], acc3[:, tc * 512:(tc + 1) * 512], sq, ALU.add,
                           [sr], [('acc3', tc)], eng=('dve' if tc == 0 else 'pool'))
            for i in range(2):
                b = 4 + i
                MM(ps[b][:], ones_f[:], acc3[:, i * 512:(i + 1) * 512], True, True, ['ones_f', ('acc3', i)], [PSR[b]])
                rstd_from(ps[b][:], rstd3[:, i * 512:(i + 1) * 512], tmp3[:, i * 512:(i + 1) * 512], D,
                          [PSR[b]], 'tmp3', ('rstd3', i))
            def ld_h3(n):
                it_, ir_, il_ = h3c.next()
                DMA('sp', it_, hbuf[n * 128:(n + 1) * 128, hf * HT:(hf + 1) * HT], [('hbuf', n, hf)], [ir_], il_)
                return it_, ir_
            hq = [ld_h3(0)]
            wq = [load_w(wrp, [(0, 512)], [w_pg_v[:, :, 0:512]])]
            for sn in range(4):
                if sn + 1 < 4:
                    wq.append(load_w(wrp, [(0, 512)], [w_pg_v[:, :, (sn + 1) * 512:(sn + 2) * 512]]))
                wt, wres = wq.pop(0)
                for n2 in range(4):
                    n = sn * 4 + n2
                    gb, eb = ((0, 1), (2, 3)) if n % 2 == 0 else ((4, 5), (6, 7))
                    if n + 1 < 16:
                        hq.append(ld_h3(n + 1))
                    for k in range(16):
                        for tc in range(2):
                            MM(ps[gb[tc]][:], wt[:, k, n2 * 128:(n2 + 1) * 128], u[:, k, tc * 512:(tc + 1) * 512],
                               k == 0, k == 15, [wres, ('u', k)], [PSR[gb[tc]]])
                    for tc in range(2):
                        for kc in range(2):
                            MM(ps[eb[tc]][:], wpp[:, kc, n * 128:(n + 1) * 128], pTb[:, kc, tc * 512:(tc + 1) * 512],
                               kc == 0, kc == 1, ['wpp', 'pTb'], [PSR[eb[tc]]])
                    it, ir = hq.pop(0)
                    ot, orr, ol = o3c.next()
                    for tc in range(2):
                        sg, sgr, _ = sg_r.next()
                        TT(sg, ps[gb[tc]][:], rstd2[:, tc * 512:(tc + 1) * 512], ALU.mult, [PSR[gb[tc]], 'rstd2'], [sgr])
                        ACT(sg, sg, AF.Sigmoid, [sgr], [sgr])
                        tt, tr, _ = t_r.next()
                        STT(tt, ps[eb[tc]][:], cst[:, C_GPO + n:C_GPO + n + 1], rstd3[:, tc * 512:(tc + 1) * 512],
                            ALU.mult, ALU.mult, [PSR[eb[tc]], ('rstd3', tc), 'cst'], [tr])
                        TT(tt, tt, sg, ALU.mult, [sgr], [tr])
                        TT(ot[:, tc * 512:(tc + 1) * 512], tt, it[:, tc * 512:(tc + 1) * 512], ALU.add,
                           [tr, ir], [orr], eng='pool')
                    DMA('sp', outT[n * 128:(n + 1) * 128, hf * HT:(hf + 1) * HT], ot, [orr], [('out', n, hf)], ol)

        p.emit()
    return nc


def _tables():
    k = np.arange(128, dtype=np.float32)[:, None]
    q = np.arange(128, dtype=np.float32)[None, :]
    sl_swa = np.asarray([2.0 ** (-8.0 * (h + 1) / 16) for h in range(16)], dtype=np.float32)
    sl_d = np.asarray([2.0 ** (-8.0 * (h + 1) / 8) for h in range(8)], dtype=np.float32)
    tb_swa = np.zeros((128, 2, 2, 2, 4, 128), np.float32)
    for kind in range(2):
        dist = (q - k) if kind == 0 else (q - k + 128.0)
        valid = (dist >= 0) & (dist < 128)
        for g in range(2):
            for par in range(2):
                for i in range(4):
                    H = 8 * g + 2 * i + par
                    tb_swa[:, kind, g, par, i, :] = np.where(valid, -sl_swa[H] * dist, NEG).astype(np.float32)
    aug = np.zeros((8, 2, 3, S), np.float32)
    t = np.arange(S)
    for h in range(8):
        aug[h, 0, 0] = sl_d[h] * (t % 128)
        aug[h, 0, 1] = 1.0
        aug[h, 0, 2] = 1.0
        aug[h, 1, 0] = 1.0
        aug[h, 1, 1] = -sl_d[h] * (t % 128)
        aug[h, 1, 2] = -sl_d[h] * 128.0 * ((t // 128) % 4)
    tb_mask = np.where(q - k >= 0, 0.0, NEG).astype(np.float32)
    return (tb_swa.reshape(128, 4096), aug.reshape(48, S), tb_mask)


def _pack_consts(g_attn, g_ffn, g_ple, g_ple_out, qn_swa, kn_swa, qn_diff, kn_diff, g_sub, sinks, lq1, lk1, lq2, lk2):
    c = np.zeros((128, NCST), np.float32)
    c[:, C_GATTN:C_GATTN + 16] = g_attn.reshape(16, 128).T
    c[:, C_GFFN:C_GFFN + 16] = g_ffn.reshape(16, 128).T
    c[:, C_GPLE:C_GPLE + 16] = g_ple.reshape(16, 128).T
    c[:, C_GPO:C_GPO + 16] = g_ple_out.reshape(16, 128).T
    c[:, C_QNS] = np.tile(qn_swa, 2)
    c[:, C_KNS] = np.tile(kn_swa, 2)
    c[:, C_QND] = np.tile(qn_diff, 2)
    c[:, C_KND] = np.tile(kn_diff, 2)
    c[:, C_GSUB] = g_sub
    for g in range(2):
        for i in range(4):
            c[0:64, C_SINK + g * 4 + i] = sinks[8 * g + 2 * i]
            c[64:128, C_SINK + g * 4 + i] = sinks[8 * g + 2 * i + 1]
    c[:, C_LQ1:C_LQ1 + 64] = lq1[None, :]
    c[:, C_LK1:C_LK1 + 64] = lk1[None, :]
    c[:, C_LQ2:C_LQ2 + 64] = lq2[None, :]
    c[:, C_LK2:C_LK2 + 64] = lk2[None, :]
    return c


def kernel(x, p, g_attn, w_in, qn_swa, kn_swa, sinks, qn_diff, kn_diff,
           lambda_q1, lambda_k1, lambda_q2, lambda_k2, g_sub, w_out,
           g_ffn, w_gate, w_up, w_down, g_ple, w_ple_gate, w_ple_proj, g_ple_out):
    f = lambda a: np.ascontiguousarray(np.asarray(a, dtype=np.float32))
    x = f(x)
    p = f(p)
    cst = _pack_consts(f(g_attn)[0], f(g_ffn)[0], f(g_ple)[0], f(g_ple_out)[0], f(qn_swa)[0], f(kn_swa)[0],
                       f(qn_diff)[0], f(kn_diff)[0], f(g_sub)[0], f(sinks)[0], f(lambda_q1)[0], f(lambda_k1)[0],
                       f(lambda_q2)[0], f(lambda_k2)[0])
    tb_swa, aug, tb_mask = _tables()
    shared = {
        "w_in": f(w_in)[0], "w_out": f(w_out)[0], "w_gate": f(w_gate)[0], "w_up": f(w_up)[0],
        "w_down": f(w_down)[0], "w_pg": f(w_ple_gate)[0], "w_pp": f(w_ple_proj)[0],
        "cst": cst, "tb_swa": tb_swa, "aug": aug, "tb_mask": tb_mask,
    }
    in_maps = []
    for b in range(8):
        m = dict(shared)
        m["xT"] = np.ascontiguousarray(x[b].T)
        m["pT"] = np.ascontiguousarray(p[0, b].T)
        in_maps.append(m)
    nc = build_nc()
    res = run_bass_kernel_spmd(nc, in_maps, core_ids=list(range(8)))
    out = np.stack([np.ascontiguousarray(np.asarray(r["outT"]).T) for r in res.results], axis=0)
    return out.astype(np.float32)
```

```python
import numpy as np
from contextlib import ExitStack
import concourse.bass as bass
import concourse.mybir as mybir
from concourse.bass_utils import run_bass_kernel_spmd

F32 = mybir.dt.float32
BF16 = mybir.dt.bfloat16
ALU = mybir.AluOpType
AF = mybir.ActivationFunctionType

S = 2048
D = 2048
DFF = 5632
NF = DFF // 128
EPS = 1e-6
NEG = -30000.0
COMPUTE = ('pe', 'act', 'dve', 'pool', 'sp')


class Prog:
    def __init__(self, nc):
        self.nc = nc
        self.engs = {e: [] for e in COMPUTE}
        self.lanes = {e: [] for e in COMPUTE}
        self.res_w = {}
        self.res_r = {}
        self.bar = set()

    def barrier(self):
        self.bar = {(ln, len(lst) - 1) for ln, lst in self.lanes.items() if lst}

    def op(self, eng, fn, reads=(), writes=(), lane=None):
        deps = set()
        for r in reads:
            w = self.res_w.get(r)
            if w is not None:
                deps.add(w)
        for w in writes:
            lw = self.res_w.get(w)
            if lw is not None:
                deps.add(lw)
            for rd in self.res_r.get(w, ()):
                deps.add(rd)
        ln = lane or eng
        if ln not in self.lanes:
            self.lanes[ln] = []
        nd = set()
        for (dl, di) in deps:
            if dl not in COMPUTE:
                if dl == ln:
                    continue
                di = len(self.lanes[dl]) - 1
            nd.add((dl, di))
        deps = nd | self.bar
        idx = len(self.lanes[ln])
        ent = dict(eng=eng, fn=fn, deps=deps, lane=ln, idx=idx, flag=(lane is not None),
                   dma=(lane is not None))
        self.lanes[ln].append(ent)
        self.engs[eng].append(ent)
        me = (ln, idx)
        for r in reads:
            self.res_r.setdefault(r, []).append(me)
        for w in writes:
            self.res_w[w] = me
            self.res_r[w] = []
        return me

    def emit(self):
        nc = self.nc
        for e in self.engs['pe']:
            e['deps'] = {d for d in e['deps'] if d[0] != 'pe'}
        for eng in COMPUTE:
            for e in self.engs[eng]:
                for (ln, i) in e['deps']:
                    self.lanes[ln][i]['flag'] = True
        for ln, lst in self.lanes.items():
            if lst:
                lst[-1]['flag'] = True
        for ln, lst in self.lanes.items():
            c = 0
            for e in lst:
                if e['flag']:
                    c += 16 if e['dma'] else 1
                e['cnt'] = c
        with ExitStack() as st:
            sems = {}
            for ln, lst in self.lanes.items():
                if lst:
                    sems[ln] = st.enter_context(nc.semaphore("s_" + ln))
            block = st.enter_context(nc.Block())
            totals = {ln: lst[-1]['cnt'] for ln, lst in self.lanes.items() if lst}

            def run(engname, engobj):
                waited = {}
                for e in self.engs[engname]:
                    for (ln, i) in sorted(e['deps']):
                        v = self.lanes[ln][i]['cnt']
                        if waited.get(ln, 0) < v:
                            engobj.wait_ge(sems[ln], v)
                            waited[ln] = v
                    ins = e['fn'](engobj)
                    if e['flag']:
                        ins.then_inc(sems[e['lane']], 16 if e['dma'] else 1)
                if engname == 'sp':
                    for ln, v in totals.items():
                        if waited.get(ln, 0) < v:
                            engobj.wait_ge(sems[ln], v)

            if self.engs['pe']:
                block.tensor(lambda eng: run('pe', eng))
            if self.engs['act']:
                block.scalar(lambda eng: run('act', eng))
            if self.engs['dve']:
                block.vector(lambda eng: run('dve', eng))
            if self.engs['pool']:
                block.gpsimd(lambda eng: run('pool', eng))
            block.sync(lambda eng: run('sp', eng))


C_GATTN, C_GFFN, C_GPLE, C_GPO = 0, 16, 32, 48
C_QNS, C_KNS, C_QND, C_KND, C_GSUB = 64, 65, 66, 67, 68
C_SINK = 69
C_LQ1, C_LK1, C_LQ2, C_LK2 = 77, 141, 205, 269
NCST = 336


def build_nc():
    nc = bass.Bass("TRN2", target_bir_lowering=False)
    dt_in = lambda n, s, d=F32: nc.dram_tensor(n, s, d, kind="ExternalInput").ap()
    xT = dt_in("xT", [D, S])
    pT = dt_in("pT", [256, S])
    w_in = dt_in("w_in", [D, 4352])
    w_out = dt_in("w_out", [D, D])
    w_gate = dt_in("w_gate", [D, DFF])
    w_up = dt_in("w_up", [D, DFF])
    w_down = dt_in("w_down", [DFF, D])
    w_pg = dt_in("w_pg", [D, D])
    w_pp = dt_in("w_pp", [256, D])
    cstd = dt_in("cst", [128, NCST])
    tb_swa = dt_in("tb_swa", [128, 4096])
    aug = dt_in("aug", [48, S])
    tb_mask = dt_in("tb_mask", [128, 128])
    outT = nc.dram_tensor("outT", [D, S], F32, kind="ExternalOutput").ap()
    hbuf = nc.dram_tensor("hbuf", [D, S], F32, kind="Internal").ap()
    ybuf = nc.dram_tensor("ybuf", [D, S], BF16, kind="Internal").ap()
    ubuf = nc.dram_tensor("ubuf", [D, S], BF16, kind="Internal").ap()

    ARENA_B = 194 * 1024

    with ExitStack() as st:
        T = lambda name, shape, dt: st.enter_context(nc.sbuf_tensor(name, shape, dt))
        arena = T("arena", [128, ARENA_B // 2], BF16)
        cst = T("cstt", [128, NCST], F32)
        drv = T("drv", [128, 32], F32)
        ones_f = T("ones_f", [128, 128], F32)
        blk_f = T("blk_f", [128, 128], F32)
        ones_b = T("ones_b", [128, 128], BF16)
        rstd_all = T("rstd_all", [128, S], F32)
        ps = [st.enter_context(nc.psum_tensor("ps%d" % i, [128, 512], F32)) for i in range(8)]
        p = Prog(nc)
        PSR = ['ps%d' % i for i in range(8)]

        class Arena:
            def __init__(self):
                self.off = 0

            def reset(self):
                self.off = 0

            def alloc(self, shape, dt):
                esz = 2 if dt == BF16 else 4
                n = int(np.prod(shape[1:]))
                nb = n * esz
                assert nb % 4 == 0
                v = arena[:, self.off // 2:(self.off + nb) // 2]
                if dt != BF16:
                    v = v.bitcast(dt)
                if len(shape) == 3:
                    v = v.rearrange("p (a b) -> p a b", a=shape[1], b=shape[2])
                elif len(shape) == 4:
                    v = v.rearrange("p (a b c) -> p a b c", a=shape[1], b=shape[2], c=shape[3])
                self.off += nb
                assert self.off <= ARENA_B, ("arena overflow", self.off)
                return v

        A = Arena()

        def alloc_at(off, shape, dt):
            sv = A.off
            A.off = off
            v = A.alloc(shape, dt)
            A.off = sv
            return v

        def MM(out, lhsT, rhs, start, stop, reads, writes):
            p.op('pe', lambda e: e.matmul(out, lhsT=lhsT, rhs=rhs, start=start, stop=stop), reads, writes)

        def ACT(out, in_, func, reads, writes, **kw):
            p.op('act', lambda e: e.activation(out=out, in_=in_, func=func, **kw), reads, writes)

        def TT(out, in0, in1, op, reads, writes, eng='dve'):
            p.op(eng, lambda e: e.tensor_tensor(out=out, in0=in0, in1=in1, op=op), reads, writes)

        def STT(out, in0, scalar, in1, op0, op1, reads, writes):
            p.op('dve', lambda e: e.scalar_tensor_tensor(out=out, in0=in0, scalar=scalar, in1=in1, op0=op0, op1=op1),
                 reads, writes)

        def TS(out, in0, s1, s2, op0, op1, reads, writes):
            p.op('dve', lambda e: e.tensor_scalar(out=out, in0=in0, scalar1=s1, scalar2=s2, op0=op0, op1=op1),
                 reads, writes)

        def DMA(eng, out, in_, reads, writes, lane):
            p.op(eng, lambda e: e.dma_start(out=out, in_=in_), reads, writes, lane=lane)

        class Ring:
            def __init__(self, name, bufs):
                self.name, self.bufs, self.i = name, bufs, 0

            def next(self):
                k = self.i % len(self.bufs)
                self.i += 1
                return self.bufs[k], "%s%d" % (self.name, k), "d_%s%d" % (self.name, k)

        DMA('sp', cst[:], cstd, [], ['cst'], 'd_cst')
        p.op('dve', lambda e: e.memset(ones_f[:], 1.0), [], ['ones_f'])
        p.op('dve', lambda e: e.memset(blk_f[:], 0.0), [], ['blk_f'])
        p.op('dve', lambda e: e.memset(blk_f[0:64, 0:64], 1.0), [], ['blk_f'])
        p.op('dve', lambda e: e.memset(blk_f[64:128, 64:128], 1.0), [], ['blk_f'])
        p.op('dve', lambda e: e.memset(ones_b[:], 1.0), [], ['ones_b'])
        ones_h = [T("ones_h%d" % i, [128, 128], BF16) for i in range(2)]
        for i in range(2):
            p.op('dve', lambda e, i=i: e.memset(ones_h[i][:], 0.0), [], ['ones_b'])
            p.op('dve', lambda e, i=i: e.memset(ones_h[i][:, i * 64:(i + 1) * 64], 1.0), [], ['ones_b'])
        TS(drv[:, 0:1], cst[:, C_QNS:C_QNS + 1], 0.125, None, ALU.mult, ALU.bypass, ['cst'], ['drv'])
        TS(drv[:, 1:2], cst[:, C_QND:C_QND + 1], 0.125, None, ALU.mult, ALU.bypass, ['cst'], ['drv'])
        TS(drv[:, 2:3], cst[:, C_GSUB:C_GSUB + 1], 0.8, None, ALU.mult, ALU.bypass, ['cst'], ['drv'])
        lam_t = T("lam_t", [128, 128], F32)
        TT(lam_t[:, 0:64], cst[:, C_LQ1:C_LQ1 + 64], cst[:, C_LK1:C_LK1 + 64], ALU.mult, ['cst'], ['lam_t'])
        TT(lam_t[:, 64:128], cst[:, C_LQ2:C_LQ2 + 64], cst[:, C_LK2:C_LK2 + 64], ALU.mult, ['cst'], ['lam_t'])
        p.op('dve', lambda e: e.reduce_sum(out=drv[:, 12:13], in_=lam_t[:, 0:64], axis=mybir.AxisListType.X),
             ['lam_t'], ['drv'])
        p.op('dve', lambda e: e.reduce_sum(out=drv[:, 13:14], in_=lam_t[:, 64:128], axis=mybir.AxisListType.X),
             ['lam_t', 'drv'], ['drv'])
        ACT(drv[:, 14:16], drv[:, 12:14], AF.Exp, ['drv'], ['drv'])
        ACT(drv[:, 4:12], cst[:, C_SINK:C_SINK + 8], AF.Exp, ['cst', 'drv'], ['drv'])
        TT(drv[:, 3:4], drv[:, 15:16], drv[:, 14:15], ALU.subtract, ['drv'], ['drv'])
        TS(drv[:, 3:4], drv[:, 3:4], -0.2, None, ALU.add, ALU.bypass, ['drv'], ['drv'])

        def rstd_from(ps_ap, out_ap, tmp_ap, n, reads, tmp_res, out_res):
            ACT(tmp_ap, ps_ap, AF.Ln, reads, [tmp_res], scale=1.0 / n, bias=EPS)
            ACT(out_ap, tmp_ap, AF.Exp, [tmp_res], [out_res], scale=-0.5)

        A.reset()
        uT = A.alloc([128, 16, S], BF16)
        markB = A.off
        xin = Ring('xin', [A.alloc([128, 16, 512], F32) for _ in range(2)])
        sqr = Ring('sqa', [A.alloc([128, 512], F32) for _ in range(4)])
        lnb = Ring('lna', [A.alloc([128, 512], F32) for _ in range(2)])
        rsb = Ring('rsa', [A.alloc([128, 512], F32) for _ in range(2)])
        xv = xT.rearrange("(kc p) s -> p kc s", p=128)
        for tc in range(4):
            xt, xr, xl = xin.next()
            DMA('sp', xt, xv[:, :, tc * 512:(tc + 1) * 512], [], [xr], xl)
            b = tc % 2
            for k in range(16):
                sq, sr, _ = sqr.next()
                ACT(sq, xt[:, k, :], AF.Square, [xr], [sr])
                MM(ps[b][:], ones_f[:], sq, k == 0, k == 15, ['ones_f', sr], [PSR[b]])
            lt, lr, _ = lnb.next()
            rt, rr, _ = rsb.next()
            rstd_from(ps[b][:], rt, lt, D, [PSR[b]], lr, rr)
            for k in range(16):
                STT(uT[:, k, tc * 512:(tc + 1) * 512], xt[:, k, :], cst[:, C_GATTN + k:C_GATTN + k + 1], rt,
                    ALU.mult, ALU.mult, [xr, rr, 'cst'], [('uT', k, tc)])
        UT_ALL = [('uT', k, tc) for k in range(16) for tc in range(4)]

        p.barrier()
        A.off = markB
        tbl = A.alloc([128, 5120], F32)
        unit_q = A.alloc([128, 4, S], BF16)
        kp = [A.alloc([128, S], BF16) for _ in range(2)]
        unit_v = A.alloc([128, 16, 128], BF16)
        unit_v2 = A.alloc([128, 16, 128], BF16)
        p.op('pool', lambda e: e.memset(kp[0][64:128, :], 0.0), [], ['unit_k'])
        p.op('pool', lambda e: e.memset(kp[1][0:64, :], 0.0), [], ['unit_k'])
        p.op('pool', lambda e: e.memset(unit_v[:, :, 64:128], 0.0), [], ['unit_v'])
        p.op('pool', lambda e: e.memset(unit_v2[:, :, 0:64], 0.0), [], ['unit_v'])
        wrB = Ring('wrB', [A.alloc([128, 16, 256], BF16) for _ in range(2)])
        wk32 = Ring('wk32_', [A.alloc([128, 512], F32) for _ in range(10)])
        wk16 = Ring('wk16_', [A.alloc([128, 512], BF16) for _ in range(4)])
        ystage = Ring('yst', [A.alloc([128, 4, S], BF16)])
        fin32 = Ring('fin32_', [A.alloc([128, 512], F32) for _ in range(10)])
        sctr = [0]

        def run_pipeline(steps, front, back, L):
            q = []
            deferred = []

            def tick():
                for d in deferred:
                    d[0] -= 1
                ready = [d for d in deferred if d[0] <= 0]
                deferred[:] = [d for d in deferred if d[0] > 0]
                for d in ready:
                    d[1]()

            for stp in steps:
                q.append((stp, front(stp)))
                if len(q) > L:
                    s0, i0 = q.pop(0)
                    for (dl, fn) in back(s0, i0):
                        deferred.append([dl, fn])
                tick()
            while q:
                s0, i0 = q.pop(0)
                for (dl, fn) in back(s0, i0):
                    deferred.append([dl, fn])
                tick()
            while deferred:
                tick()
        w_in_v = w_in.rearrange("(kc p) n -> p kc n", p=128)

        zsets = [(0, 1), (2, 3), (4, 5)]
        zctr = [0]
        ssb = [0]
        pending = []

        def flush_one():
            (zb, dsts, dres, gcol, blk) = pending.pop(0)
            for j in range(2):
                sq, sr = zb[2 + j]
                sb = 6 + (ssb[0] % 2)
                ssb[0] += 1
                MM(ps[sb][:], blk[:], sq, True, True, ['blk_f', 'ones_f', sr], [PSR[sb]])
                lt, lr, _ = wk32.next()
                rt, rr, _ = wk32.next()
                rstd_from(ps[sb][:], rt, lt, 64 if blk is blk_f else 128, [PSR[sb]], lr, rr)
                for (r0, r1, dap) in dsts[j]:
                    STT(dap, ps[zb[j]][r0:r1, :], gcol[r0:r1, :], rt[r0:r1, :], ALU.mult, ALU.mult,
                        [PSR[zb[j]], rr, 'drv', 'cst'], [dres])

        def proj_fm_norm(wt, wres, c0, gcol, dst_fn, dres):
            for hc in range(2):
                zb = zsets[zctr[0] % 3]
                zctr[0] += 1
                for k in range(16):
                    for j in range(2):
                        tcx = hc * 2 + j
                        MM(ps[zb[j]][:], wt[:, k, c0:c0 + 128], uT[:, k, tcx * 512:(tcx + 1) * 512],
                           k == 0, k == 15, [wres, ('uT', k, tcx)], [PSR[zb[j]]])
                sqs = []
                for j in range(2):
                    sq, sr, _ = wk32.next()
                    ACT(sq, ps[zb[j]][:], AF.Square, [PSR[zb[j]]], [sr])
                    sqs.append((sq, sr))
                if pending:
                    flush_one()
                pending.append(((zb[0], zb[1], sqs[0], sqs[1]),
                                [dst_fn(hc * 2), dst_fn(hc * 2 + 1)], dres, gcol, blk_f))

        def flush_all():
            while pending:
                flush_one()

        def proj_tm(wt, wres, c0, ncol, dst, dres):
            per = 512 // ncol
            nb = 16 // per
            for tb in range(16):
                bk = tb // per
                co = (tb % per) * ncol
                for k in range(16):
                    MM(ps[bk][:, co:co + ncol], uT[:, k, tb * 128:(tb + 1) * 128], wt[:, k, c0:c0 + ncol],
                       k == 0, k == 15, [wres, ('uT', k, tb // 4)], [PSR[bk]])
            ei = 0
            for bk in range(nb):
                src = ps[bk][:].rearrange("p (a b) -> p a b", a=per, b=ncol)
                for (dt_, dc, sc, wd) in dst:
                    dv = dt_[:, bk * per:(bk + 1) * per, dc:dc + wd]
                    sv = src[:, :, sc:sc + wd]
                    if ei % 2 == 0:
                        p.op('dve', lambda e, dv=dv, sv=sv: e.tensor_copy(out=dv, in_=sv), [PSR[bk]], [dres])
                    else:
                        ACT(dv, sv, AF.Copy, [PSR[bk]], [dres])
                    ei += 1

        def load_w(ring, dst_cols, src_ap_list):
            wt, wres, wl = ring.next()
            for (c0, c1), src in zip(dst_cols, src_ap_list):
                DMA('pool', wt[:, :, c0:c1], src, [], [wres], wl)
            return wt, wres

        DMA('sp', tbl[:, 0:4096], tb_swa, [], ['tbl'], 'd_tbl')
        tblv = tbl[:, 0:4096].rearrange("p (a b) -> p a b", a=8, b=512)
        for g in range(2):
            yt, yr, yl = ystage.next()
            for half in range(2):
                c = g * 512 + half * 256
                wt, wres = load_w(wrB, [(0, 256)], [w_in_v[:, :, c:c + 256]])
                for cc in range(2):
                    qi = half * 2 + cc
                    proj_fm_norm(wt, wres, cc * 128, drv[:, 0:1],
                                 lambda tc, qi=qi: [(0, 128, unit_q[:, qi, tc * 512:(tc + 1) * 512])], 'unit_q')
            kc0 = 1024 + g * 64
            vc0 = 1152 + g * 64
            wt, wres = load_w(wrB, [(0, 64), (64, 128), (128, 192)],
                              [w_in_v[:, :, kc0:kc0 + 64], w_in_v[:, :, kc0:kc0 + 64], w_in_v[:, :, vc0:vc0 + 64]])
            proj_fm_norm(wt, wres, 0, cst[:, C_KNS:C_KNS + 1],
                         lambda tc: [(0, 64, kp[0][0:64, tc * 512:(tc + 1) * 512]), (64, 128, kp[1][64:128, tc * 512:(tc + 1) * 512])], 'unit_k')
            flush_all()
            proj_tm(wt, wres, 128, 64, [(unit_v, 0, 0, 64), (unit_v2, 64, 0, 64)], 'unit_v')
            esb = drv[:, 4 + g * 4:8 + g * 4]
            es_bc = bass.AP(esb.tensor, esb.offset, [list(esb.ap[0]), [1, 4], [0, 128]])
            steps = []
            for n in range(16):
                kbs = [n - 1, n] if n > 0 else [n]
                for par in range(2):
                    for kb in kbs:
                        steps.append((n, par, kb, par == 0 and kb == kbs[0], par == 1 and kb == kbs[-1],
                                      par == 1 and kb == kbs[-1]))

            def swa_front(stp, g=g):
                (n, par, kb, first, last, fin) = stp
                kind = 0 if kb == n else 1
                sb = sctr[0] % 4
                sctr[0] += 1
                MM(ps[sb][:], kp[par][:, kb * 128:(kb + 1) * 128],
                   unit_q[:, :, n * 128:(n + 1) * 128], True, True,
                   ['unit_k', 'unit_q'], [PSR[sb]])
                tt, tr, _ = wk32.next()
                TT(tt, ps[sb][:], tblv[:, kind * 4 + g * 2 + par, :], ALU.add, [PSR[sb], 'tbl'], [tr])
                et, er, _ = wk16.next()
                ACT(et, tt, AF.Exp, [tr], [er])
                return (et, er)

            def swa_back(stp, info, yt=yt, yr=yr):
                (n, par, kb, first, last, fin) = stp
                (et, er) = info
                pvb = 4 + (n % 2)
                dnb = 6 + (n % 2)
                MM(ps[pvb][:], (unit_v if par == 0 else unit_v2)[:, kb, :], et, first, last,
                   ['unit_v', er], [PSR[pvb]])
                MM(ps[dnb][:], ones_h[par][:], et, first, last,
                   ['ones_b', er], [PSR[dnb]])
                if not fin:
                    return []
                dd, dr, _ = fin32.next()
                lt, lr, _ = fin32.next()
                rt, rr, _ = fin32.next()

                def stage_a():
                    TT(dd.rearrange("p (a b) -> p a b", a=4, b=128),
                       ps[dnb][:].rearrange("p (a b) -> p a b", a=4, b=128),
                       es_bc, ALU.add, [PSR[dnb], 'drv'], [dr])
                    ACT(lt, dd, AF.Ln, [dr], [lr])
                    ACT(rt, lt, AF.Exp, [lr], [rr], scale=-1.0)

                def stage_b():
                    TT(yt[:, :, n * 128:(n + 1) * 128], ps[pvb][:].rearrange("p (a b) -> p a b", a=4, b=128),
                       rt.rearrange("p (a b) -> p a b", a=4, b=128), ALU.mult, [PSR[pvb], rr], [yr])
                return [(1, stage_a), (3, stage_b)]

            run_pipeline(steps, swa_front, swa_back, 3)
            for i in range(4):
                ch = g * 4 + i
                DMA('sp', ybuf[ch * 128:(ch + 1) * 128, :], yt[:, i, :], [yr], [('ybuf', ch)], yl)

        maskt = tbl[:, 0:128]
        DMA('sp', maskt, tb_mask, [], ['tbl'], 'd_tbl')
        maskb = tbl[:, 128:192].bitcast(BF16)
        TS(maskb, maskt, -1.0, None, ALU.is_ge, ALU.bypass, ['tbl'], ['maskb'])
        qp = [unit_q[:, 0, :], unit_q[:, 1, :]]
        p.op('dve', lambda e: e.memset(unit_q[:, 0:2, :], 0.0), [], ['unit_q'])
        aug_v = aug.rearrange("(h a r) s -> h a r s", h=8, a=2, r=3)
        for h in range(8):
            slope = 2.0 ** (-(h + 1))
            yt, yr, yl = ystage.next()
            qc0 = 1280 + h * 128
            kc0 = 2304 + h * 128
            vc0 = 3328 + h * 128
            wt, wres = load_w(wrB, [(0, 128), (128, 256)], [w_in_v[:, :, qc0:qc0 + 128], w_in_v[:, :, kc0:kc0 + 128]])
            DMA('pool', kp[0][64:67, :], aug_v[h, 0], [], ['aug', 'unit_k'], 'd_aug')
            DMA('pool', kp[1][0:3, :], aug_v[h, 0], [], ['aug', 'unit_k'], 'd_aug')
            DMA('pool', qp[0][64:67, :], aug_v[h, 1], [], ['aug', 'unit_q'], 'd_aug')
            DMA('pool', qp[1][0:3, :], aug_v[h, 1], [], ['aug', 'unit_q'], 'd_aug')
            proj_fm_norm(wt, wres, 0, drv[:, 1:2],
                         lambda tc: [(0, 64, qp[0][0:64, tc * 512:(tc + 1) * 512]),
                                     (64, 128, qp[1][64:128, tc * 512:(tc + 1) * 512])], 'unit_q')
            proj_fm_norm(wt, wres, 128, cst[:, C_KND:C_KND + 1],
                         lambda tc: [(0, 64, kp[0][0:64, tc * 512:(tc + 1) * 512]),
                                     (64, 128, kp[1][64:128, tc * 512:(tc + 1) * 512])], 'unit_k')
            flush_all()
            if h % 2 == 0:
                wt, wres = load_w(wrB, [(0, 256)], [w_in_v[:, :, vc0:vc0 + 256]])
                proj_tm(wt, wres, 0, 256, [(unit_v, 0, 0, 128), (unit_v2, 0, 128, 128)], 'unit_v')
            vd = unit_v if h % 2 == 0 else unit_v2
            steps = []
            for c in range(4):
                for m in range(2):
                    pi = c * 2 + m
                    for j in range(4 * c + 4):
                        steps.append((c, m, j, 4 + 2 * (pi % 2), 5 + 2 * (pi % 2)))
            ats = {}

            def diff_front(stp, h=h, slope=slope):
                (c, m, j, pvb, dnb) = stp
                diag = j >= 4 * c
                jk = j - 4 * c if diag else 0
                N = 512 - jk * 128
                off = 512 - N
                q0 = c * 512 + off
                sb = sctr[0] % 4
                sctr[0] += 1
                MM(ps[sb][:, 0:N], kp[m][:, j * 128:(j + 1) * 128],
                   qp[m][:, q0:q0 + N], True, True, ['unit_k', 'unit_q', 'aug'], [PSR[sb]])
                cb = float(-slope * 128.0 * (4 * c - j))
                et, er, _ = wk16.next()
                ACT(et[:, 0:N], ps[sb][:, 0:N], AF.Exp, [PSR[sb]], [er], bias=cb)
                if diag:
                    TT(et[:, 0:128], et[:, 0:128], maskb, ALU.mult, ['maskb'], [er])
                return (et, er, N, off)

            def diff_back(stp, info, yt=yt, yr=yr, vd=vd):
                (c, m, j, pvb, dnb) = stp
                (et, er, N, off) = info
                last_j = 4 * c + 3
                MM(ps[pvb][:, off:512], vd[:, j, :], et[:, 0:N], j == 0, j == last_j, ['unit_v', er], [PSR[pvb]])
                MM(ps[dnb][:, off:512], ones_b[:], et[:, 0:N], j == 0, j == last_j, ['ones_b', er], [PSR[dnb]])
                if j != last_j:
                    return []
                lt, lr, _ = fin32.next()
                rt, rr, _ = fin32.next()
                at, ar, _ = fin32.next()
                ats[(c, m)] = (at, ar)

                def stage_a():
                    ACT(lt, ps[dnb][:], AF.Ln, [PSR[dnb]], [lr])
                    ACT(rt, lt, AF.Exp, [lr], [rr], scale=-1.0)

                def stage_b():
                    TT(at, ps[pvb][:], rt, ALU.mult, [PSR[pvb], rr], [ar])
                outl = [(1, stage_a), (3, stage_b)]
                if m == 1:
                    od, odr, _ = fin32.next()
                    sq, sr, _ = fin32.next()
                    lt2, lr2, _ = fin32.next()
                    rt2, rr2, _ = fin32.next()

                    def stage_c():
                        a0, a0r = ats[(c, 0)]
                        a1, a1r = ats[(c, 1)]
                        STT(od, a1, drv[:, 3:4], a0, ALU.mult, ALU.add, [a0r, a1r, 'drv'], [odr])
                        ACT(sq, od, AF.Square, [odr], [sr])

                    def stage_d():
                        sb = sctr[0] % 4
                        sctr[0] += 1
                        MM(ps[sb][:], ones_f[:], sq, True, True, ['ones_f', sr], [PSR[sb]])
                        rstd_from(ps[sb][:], rt2, lt2, 128, [PSR[sb]], lr2, rr2)

                    def stage_e():
                        STT(yt[:, 0, c * 512:(c + 1) * 512], od, drv[:, 2:3], rt2, ALU.mult, ALU.mult,
                            [odr, rr2, 'drv'], [yr])
                    outl += [(4, stage_c), (5, stage_d), (7, stage_e)]
                return outl

            run_pipeline(steps, diff_front, diff_back, 3)
            ch = 8 + h
            DMA('sp', ybuf[ch * 128:(ch + 1) * 128, :], yt[:, 0, :], [yr], [('ybuf', ch)], yl)

        p.barrier()
        A.reset()
        yT = A.alloc([128, 16, S], BF16)
        xc = Ring('xc', [A.alloc([128, S], F32) for _ in range(2)])
        hc_r = Ring('hc', [A.alloc([128, S], F32) for _ in range(2)])
        sq_r = Ring('sqc', [A.alloc([128, S], F32) for _ in range(2)])
        accsq = A.alloc([128, S], F32)
        wrC = Ring('wrC', [A.alloc([128, 16, 512], BF16) for _ in range(2)])
        tmpc = A.alloc([128, S], F32)
        ust_r = Ring('ust', [A.alloc([128, S], BF16) for _ in range(2)])
        yv = ybuf.rearrange("(kc p) s -> p kc s", p=128)
        for gk in range(4):
            DMA('sp', yT[:, gk * 4:(gk + 1) * 4, :], yv[:, gk * 4:(gk + 1) * 4, :],
                [('ybuf', k) for k in range(gk * 4, gk * 4 + 4)], [('yT', gk)], 'd_yT%d' % gk)
        w_out_v = w_out.rearrange("(kc p) n -> p kc n", p=128)
        def ld_x(n):
            xt, xr, xl = xc.next()
            DMA('sp', xt, xT[n * 128:(n + 1) * 128, :], [], [xr], xl)
            return xt, xr
        xq = [ld_x(0)]
        wq = [load_w(wrC, [(0, 512)], [w_out_v[:, :, 0:512]])]
        for sn in range(4):
            if sn + 1 < 4:
                wq.append(load_w(wrC, [(0, 512)], [w_out_v[:, :, (sn + 1) * 512:(sn + 2) * 512]]))
            wt, wres = wq.pop(0)
            for n4 in range(4):
                n = sn * 4 + n4
                banks = (0, 1, 2, 3) if n % 2 == 0 else (4, 5, 6, 7)
                if n + 1 < 16:
                    xq.append(ld_x(n + 1))
                for k in range(16):
                    for tc in range(4):
                        MM(ps[banks[tc]][:], wt[:, k, n4 * 128:(n4 + 1) * 128], yT[:, k, tc * 512:(tc + 1) * 512],
                           k == 0, k == 15, [wres, ('yT', k // 4)], [PSR[banks[tc]]])
                xt, xr = xq.pop(0)
                ht, hr, hl = hc_r.next()
                for tc in range(4):
                    TT(ht[:, tc * 512:(tc + 1) * 512], ps[banks[tc]][:], xt[:, tc * 512:(tc + 1) * 512], ALU.add,
                       [PSR[banks[tc]], xr], [hr])
                DMA('sp', hbuf[n * 128:(n + 1) * 128, :], ht, [hr], [('hbuf', n, 0), ('hbuf', n, 1)], hl)
                ut, utr, utl = ust_r.next()
                ACT(ut, ht, AF.Copy, [hr, 'cst'], [utr], scale=cst[:, C_GFFN + n:C_GFFN + n + 1])
                DMA('sp', ubuf[n * 128:(n + 1) * 128, :], ut, [utr], [('ubuf', n)], utl)
                if n == 0:
                    ACT(accsq, ht, AF.Square, [hr], ['accsq'])
                else:
                    sq, sr, _ = sq_r.next()
                    ACT(sq, ht, AF.Square, [hr], [sr])
                    TT(accsq, accsq, sq, ALU.add, [sr], ['accsq'], eng='pool')

        def finish_rstd(acc_ap, acc_res, ntok, tmp_ap, out_ap, out_res, banks):
            for i in range(ntok // 512):
                b = banks[i]
                MM(ps[b][:], ones_f[:], acc_ap[:, i * 512:(i + 1) * 512], True, True, ['ones_f', acc_res], [PSR[b]])
                rstd_from(ps[b][:], out_ap[:, i * 512:(i + 1) * 512], tmp_ap[:, i * 512:(i + 1) * 512], D,
                          [PSR[b]], 'tmp_rs', out_res)

        finish_rstd(accsq, 'accsq', S, tmpc, rstd_all, 'rstd_all', (0, 1, 2, 3))

        w_gate_v = w_gate.rearrange("(kc p) n -> p kc n", p=128)
        w_up_v = w_up.rearrange("(kc p) n -> p kc n", p=128)
        w_down_v = w_down.rearrange("(f p) n -> p f n", p=128)
        w_pg_v = w_pg.rearrange("(kc p) n -> p kc n", p=128)
        w_pp_v = w_pp.rearrange("(kc p) n -> p kc n", p=128)
        pT_v = pT.rearrange("(kc p) s -> p kc s", p=128)
        HT = 1024

        def make_u(u, hf, gc0, rstd_ap, rstd_res, hin):
            for k in range(16):
                it, ir, il = hin.next()
                DMA('sp', it, hbuf[k * 128:(k + 1) * 128, hf * HT:(hf + 1) * HT], [('hbuf', k, hf)], [ir], il)
                STT(u[:, k, :], it, cst[:, gc0 + k:gc0 + k + 1], rstd_ap, ALU.mult, ALU.mult,
                    [ir, rstd_res, 'cst'], [('u', k)])

        KB = 1024
        wrg = Ring('wrg', [alloc_at((124 + 8 * i) * KB, [128, 16, 256], BF16) for i in range(2)])
        wru = Ring('wru', [alloc_at((140 + 8 * i) * KB, [128, 16, 256], BF16) for i in range(2)])
        pre_w = None
        for hf in range(2):
            p.barrier()
            A.reset()
            u = A.alloc([128, 16, HT], BF16)
            aT = A.alloc([128, NF, HT], BF16)
            markW = A.off
            assert markW == 120 * KB
            ubv = ubuf.rearrange("(kc p) s -> p kc s", p=128)
            for gk in range(4):
                DMA('sp', u[:, gk * 4:(gk + 1) * 4, :], ubv[:, gk * 4:(gk + 1) * 4, hf * HT:(hf + 1) * HT],
                    [('ubuf', k) for k in range(gk * 4, gk * 4 + 4)], [('u', k) for k in range(gk * 4, gk * 4 + 4)],
                    'd_u%d' % gk)
            rs2 = rstd_all[:, hf * HT:(hf + 1) * HT]
            sil = Ring('sil', [alloc_at((156 + 2 * i) * KB, [128, 512], F32) for i in range(8)])
            wrd = Ring('wrd', [alloc_at(172 * KB, [128, NF, 256], BF16), alloc_at(148 * KB, [128, NF, 256], BF16)])

            def ld_wd(ng):
                wt_, wres_, wl_ = wrd.next()
                DMA('pool', wt_, w_down_v[:, :, ng * 256:(ng + 1) * 256], [], [wres_], wl_)
                return wt_, wres_
            for fp in range(NF // 2):
                if fp == 0 and pre_w is not None:
                    (wg, wgr), (wu, wur) = pre_w
                else:
                    wg, wgr = load_w(wrg, [(0, 256)], [w_gate_v[:, :, fp * 256:(fp + 1) * 256]])
                    wu, wur = load_w(wru, [(0, 256)], [w_up_v[:, :, fp * 256:(fp + 1) * 256]])
                for f2 in range(2):
                    f = fp * 2 + f2
                    gb, ub = ((0, 1), (2, 3)) if f % 2 == 0 else ((4, 5), (6, 7))
                    for k in range(16):
                        for tc in range(2):
                            MM(ps[gb[tc]][:], wg[:, k, f2 * 128:(f2 + 1) * 128], u[:, k, tc * 512:(tc + 1) * 512],
                               k == 0, k == 15, [wgr, ('u', k)], [PSR[gb[tc]]])
                    for k in range(16):
                        for tc in range(2):
                            MM(ps[ub[tc]][:], wu[:, k, f2 * 128:(f2 + 1) * 128], u[:, k, tc * 512:(tc + 1) * 512],
                               k == 0, k == 15, [wur, ('u', k)], [PSR[ub[tc]]])
                    for tc in range(2):
                        rsl = rs2[:, tc * 512:(tc + 1) * 512]
                        t1, t1r, _ = sil.next()
                        TT(t1, ps[gb[tc]][:], rsl, ALU.mult, [PSR[gb[tc]], 'rstd_all'], [t1r])
                        stt, sr, _ = sil.next()
                        ACT(stt, t1, AF.Silu, [t1r], [sr])
                        t2, t2r, _ = sil.next()
                        TT(t2, ps[ub[tc]][:], rsl, ALU.mult, [PSR[ub[tc]], 'rstd_all'], [t2r])
                        TT(aT[:, f, tc * 512:(tc + 1) * 512], stt, t2, ALU.mult, [sr, t2r], [('aT', f)])
            wq = [ld_wd(0)]
            p.barrier()
            A.off = markW
            rstd2 = A.alloc([128, HT], F32)
            h1c = Ring('h1c', [A.alloc([128, HT], F32) for _ in range(2)])
            h2c = Ring('h2c', [A.alloc([128, HT], F32) for _ in range(2)])
            sqd = Ring('sqd', [A.alloc([128, HT], F32) for _ in range(1)])
            acc2 = A.alloc([128, HT], F32)
            assert A.off <= 148 * KB
            tmp2 = sqd.bufs[0]

            def ld_h1(n):
                it_, ir_, il_ = h1c.next()
                DMA('sp', it_, hbuf[n * 128:(n + 1) * 128, hf * HT:(hf + 1) * HT], [('hbuf', n, hf)], [ir_], il_)
                return it_, ir_
            hq = [ld_h1(0)]
            for ng in range(8):
                if ng + 1 < 8:
                    wq.append(ld_wd(ng + 1))
                wd, wdr0 = wq.pop(0)
                for n2 in range(2):
                    n = ng * 2 + n2
                    wdr = wdr0
                    if n + 1 < 16:
                        hq.append(ld_h1(n + 1))
                    bb = ((0, 1), (2, 3), (4, 5), (6, 7))[n % 4]
                    for f in range(NF):
                        for tc in range(2):
                            MM(ps[bb[tc]][:], wd[:, f, n2 * 128:(n2 + 1) * 128], aT[:, f, tc * 512:(tc + 1) * 512],
                               f == 0, f == NF - 1, [wdr, ('aT', f)], [PSR[bb[tc]]])
                    it, ir = hq.pop(0)
                    ot, orr, ol = h2c.next()
                    for tc in range(2):
                        TT(ot[:, tc * 512:(tc + 1) * 512], ps[bb[tc]][:], it[:, tc * 512:(tc + 1) * 512], ALU.add,
                           [PSR[bb[tc]], ir], [orr])
                    DMA('sp', hbuf[n * 128:(n + 1) * 128, hf * HT:(hf + 1) * HT], ot, [orr], [('hbuf', n, hf)], ol)
                    ACT(u[:, n, :], ot, AF.Copy, [orr, 'cst'], [('u', n)], scale=cst[:, C_GPLE + n:C_GPLE + n + 1])
                    if n == 0:
                        ACT(acc2, ot, AF.Square, [orr], ['acc2'])
                    else:
                        sq, sr, _ = sqd.next()
                        ACT(sq, ot, AF.Square, [orr], [sr])
                        TT(acc2, acc2, sq, ALU.add, [sr], ['acc2'], eng='pool')
            finish_rstd(acc2, 'acc2', HT, tmp2, rstd2, 'rstd2', (0, 1))
            p.barrier()
            A.off = 32 * 1024
            pTb = A.alloc([128, 2, HT], BF16)
            wpp = A.alloc([128, 2, D], BF16)
            wrp = Ring('wrp', [A.alloc([128, 16, 512], BF16) for _ in range(2)])
            acc3 = A.alloc([128, HT], F32)
            rstd3 = A.alloc([128, HT], F32)
            tmp3 = A.alloc([128, HT], F32)
            sq3 = Ring('sq3_', [A.alloc([128, 512], F32) for _ in range(2)])
            sg_r = Ring('sg', [A.alloc([128, 512], F32) for _ in range(2)])
            t_r = Ring('tpl', [A.alloc([128, 512], F32) for _ in range(2)])
            h3c = Ring('h3c', [A.alloc([128, HT], F32) for _ in range(2)])
            o3c = Ring('o3c', [A.alloc([128, HT], F32) for _ in range(2)])
            assert A.off <= markW
            DMA('pool', pTb, pT_v[:, :, hf * HT:(hf + 1) * HT], [], ['pTb'], 'd_pTb')
            DMA('pool', wpp, w_pp_v, [], ['wpp'], 'd_wpp')
            for n in range(16):
                for tc in range(2):
                    b = (n * 2 + tc) % 4
                    for kc in range(2):
                        MM(ps[b][:], wpp[:, kc, n * 128:(n + 1) * 128], pTb[:, kc, tc * 512:(tc + 1) * 512],
                           kc == 0, kc == 1, ['wpp', 'pTb'], [PSR[b]])
                    if n == 0:
                        ACT(acc3[:, tc * 512:(tc + 1) * 512], ps[b][:], AF.Square, [PSR[b]], [('acc3', tc)])
                    else:
                        sq, sr, _ = sq3.next()
                        ACT(sq, ps[b][:], AF.Square, [PSR[b]], [sr])
                        TT(acc3[:, tc * 512:(tc + 1) * 512], acc3[:, tc * 512:(tc + 1) * 512], sq, ALU.add,
                           [sr], [('acc3', tc)], eng=('dve' if tc == 0 else 'pool'))
            for i in range(2):
                b = 4 + i
                MM(ps[b][:], ones_f[:], acc3[:, i * 512:(i + 1) * 512], True, True, ['ones_f', ('acc3', i)], [PSR[b]])
                rstd_from(ps[b][:], rstd3[:, i * 512:(i + 1) * 512], tmp3[:, i * 512:(i + 1) * 512], D,
                          [PSR[b]], 'tmp3', ('rstd3', i))
            def ld_h3(n):
                it_, ir_, il_ = h3c.next()
                DMA('sp', it_, hbuf[n * 128:(n + 1) * 128, hf * HT:(hf + 1) * HT], [('hbuf', n, hf)], [ir_], il_)
                return it_, ir_
            hq = [ld_h3(0)]
            wq = [load_w(wrp, [(0, 512)], [w_pg_v[:, :, 0:512]])]
            for sn in range(4):
                if sn + 1 < 4:
                    wq.append(load_w(wrp, [(0, 512)], [w_pg_v[:, :, (sn + 1) * 512:(sn + 2) * 512]]))
                wt, wres = wq.pop(0)
                for n2 in range(4):
                    n = sn * 4 + n2
                    gb, eb = ((0, 1), (2, 3)) if n % 2 == 0 else ((4, 5), (6, 7))
                    if n + 1 < 16:
                        hq.append(ld_h3(n + 1))
                    for k in range(16):
                        for tc in range(2):
                            MM(ps[gb[tc]][:], wt[:, k, n2 * 128:(n2 + 1) * 128], u[:, k, tc * 512:(tc + 1) * 512],
                               k == 0, k == 15, [wres, ('u', k)], [PSR[gb[tc]]])
                    for tc in range(2):
                        for kc in range(2):
                            MM(ps[eb[tc]][:], wpp[:, kc, n * 128:(n + 1) * 128], pTb[:, kc, tc * 512:(tc + 1) * 512],
                               kc == 0, kc == 1, ['wpp', 'pTb'], [PSR[eb[tc]]])
                    it, ir = hq.pop(0)
                    ot, orr, ol = o3c.next()
                    for tc in range(2):
                        sg, sgr, _ = sg_r.next()
                        TT(sg, ps[gb[tc]][:], rstd2[:, tc * 512:(tc + 1) * 512], ALU.mult, [PSR[gb[tc]], 'rstd2'], [sgr])
                        ACT(sg, sg, AF.Sigmoid, [sgr], [sgr])
                        tt, tr, _ = t_r.next()
                        STT(tt, ps[eb[tc]][:], cst[:, C_GPO + n:C_GPO + n + 1], rstd3[:, tc * 512:(tc + 1) * 512],
                            ALU.mult, ALU.mult, [PSR[eb[tc]], ('rstd3', tc), 'cst'], [tr])
                        TT(tt, tt, sg, ALU.mult, [sgr], [tr])
                        TT(ot[:, tc * 512:(tc + 1) * 512], tt, it[:, tc * 512:(tc + 1) * 512], ALU.add,
                           [tr, ir], [orr], eng='pool')
                    DMA('sp', outT[n * 128:(n + 1) * 128, hf * HT:(hf + 1) * HT], ot, [orr], [('out', n, hf)], ol)
            if hf == 0:
                pre_w = (load_w(wrg, [(0, 256)], [w_gate_v[:, :, 0:256]]),
                         load_w(wru, [(0, 256)], [w_up_v[:, :, 0:256]]))
            else:
                pre_w = None

        p.emit()
    return nc


def _tables():
    k = np.arange(128, dtype=np.float32)[:, None]
    q = np.arange(128, dtype=np.float32)[None, :]
    sl_swa = np.asarray([2.0 ** (-8.0 * (h + 1) / 16) for h in range(16)], dtype=np.float32)
    sl_d = np.asarray([2.0 ** (-8.0 * (h + 1) / 8) for h in range(8)], dtype=np.float32)
    tb_swa = np.zeros((128, 2, 2, 2, 4, 128), np.float32)
    for kind in range(2):
        dist = (q - k) if kind == 0 else (q - k + 128.0)
        valid = (dist >= 0) & (dist < 128)
        for g in range(2):
            for par in range(2):
                for i in range(4):
                    H = 8 * g + 2 * i + par
                    tb_swa[:, kind, g, par, i, :] = np.where(valid, -sl_swa[H] * dist, NEG).astype(np.float32)
    aug = np.zeros((8, 2, 3, S), np.float32)
    t = np.arange(S)
    for h in range(8):
        aug[h, 0, 0] = sl_d[h] * (t % 128)
        aug[h, 0, 1] = 1.0
        aug[h, 0, 2] = 1.0
        aug[h, 1, 0] = 1.0
        aug[h, 1, 1] = -sl_d[h] * (t % 128)
        aug[h, 1, 2] = -sl_d[h] * 128.0 * ((t // 128) % 4)
    tb_mask = np.where(q - k >= 0, 0.0, NEG).astype(np.float32)
    return (tb_swa.reshape(128, 4096), aug.reshape(48, S), tb_mask)


def _pack_consts(g_attn, g_ffn, g_ple, g_ple_out, qn_swa, kn_swa, qn_diff, kn_diff, g_sub, sinks, lq1, lk1, lq2, lk2):
    c = np.zeros((128, NCST), np.float32)
    c[:, C_GATTN:C_GATTN + 16] = g_attn.reshape(16, 128).T
    c[:, C_GFFN:C_GFFN + 16] = g_ffn.reshape(16, 128).T
    c[:, C_GPLE:C_GPLE + 16] = g_ple.reshape(16, 128).T
    c[:, C_GPO:C_GPO + 16] = g_ple_out.reshape(16, 128).T
    c[:, C_QNS] = np.tile(qn_swa, 2)
    c[:, C_KNS] = np.tile(kn_swa, 2)
    c[:, C_QND] = np.tile(qn_diff, 2)
    c[:, C_KND] = np.tile(kn_diff, 2)
    c[:, C_GSUB] = g_sub
    for g in range(2):
        for i in range(4):
            c[0:64, C_SINK + g * 4 + i] = sinks[8 * g + 2 * i]
            c[64:128, C_SINK + g * 4 + i] = sinks[8 * g + 2 * i + 1]
    c[:, C_LQ1:C_LQ1 + 64] = lq1[None, :]
    c[:, C_LK1:C_LK1 + 64] = lk1[None, :]
    c[:, C_LQ2:C_LQ2 + 64] = lq2[None, :]
    c[:, C_LK2:C_LK2 + 64] = lk2[None, :]
    return c


def kernel(x, p, g_attn, w_in, qn_swa, kn_swa, sinks, qn_diff, kn_diff,
           lambda_q1, lambda_k1, lambda_q2, lambda_k2, g_sub, w_out,
           g_ffn, w_gate, w_up, w_down, g_ple, w_ple_gate, w_ple_proj, g_ple_out):
    f = lambda a: np.ascontiguousarray(np.asarray(a, dtype=np.float32))
    x = f(x)
    p = f(p)
    cst = _pack_consts(f(g_attn)[0], f(g_ffn)[0], f(g_ple)[0], f(g_ple_out)[0], f(qn_swa)[0], f(kn_swa)[0],
                       f(qn_diff)[0], f(kn_diff)[0], f(g_sub)[0], f(sinks)[0], f(lambda_q1)[0], f(lambda_k1)[0],
                       f(lambda_q2)[0], f(lambda_k2)[0])
    tb_swa, aug, tb_mask = _tables()
    shared = {
        "w_in": f(w_in)[0], "w_out": f(w_out)[0], "w_gate": f(w_gate)[0], "w_up": f(w_up)[0],
        "w_down": f(w_down)[0], "w_pg": f(w_ple_gate)[0], "w_pp": f(w_ple_proj)[0],
        "cst": cst, "tb_swa": tb_swa, "aug": aug, "tb_mask": tb_mask,
    }
    in_maps = []
    for b in range(8):
        m = dict(shared)
        m["xT"] = np.ascontiguousarray(x[b].T)
        m["pT"] = np.ascontiguousarray(p[0, b].T)
        in_maps.append(m)
    nc = build_nc()
    res = run_bass_kernel_spmd(nc, in_maps, core_ids=list(range(8)))
    out = np.stack([np.ascontiguousarray(np.asarray(r["outT"]).T) for r in res.results], axis=0)
    return out.astype(np.float32)
```
